# Optimizing a Trainium2 kernel written in Bass

```python
import math
import jax, jax.numpy as jnp
from jax import lax
import numpy as np

D_MODEL = 1024
BATCH = 16
SEQ = 2048
DEPTH = 1
DEC_BATCH = 32
DEC_SEQ = 64
PAST_LEN = 1024

CHUNK = 64
QBLK = 128
N_HEADS = 8
N_KV_HEADS = 2
HEAD_DIM = 64
ATTN_WIDTH = N_HEADS * HEAD_DIM
KV_WIDTH = N_KV_HEADS * HEAD_DIM
ROT_DIM = HEAD_DIM // 4
ROPE_THETA = 500000.0
N_IDX_HEADS = 8
IDX_DIM = 64
TOPK_MAX = 256
SSM_WIDTH = D_MODEL // 2
SSM_GROUP = 16
N_SSM_GROUPS = SSM_WIDTH // SSM_GROUP
SSM_STATE = 64
NORM_EPS = 1e-6
SPLITS = (ATTN_WIDTH, KV_WIDTH, KV_WIDTH, N_IDX_HEADS * IDX_DIM, IDX_DIM, N_IDX_HEADS,
          ATTN_WIDTH, SSM_WIDTH, SSM_WIDTH, D_MODEL, D_MODEL)
IN_WIDTH = (2 * ATTN_WIDTH + 2 * KV_WIDTH + N_IDX_HEADS * IDX_DIM + IDX_DIM + N_IDX_HEADS
            + 2 * SSM_WIDTH + 2 * D_MODEL)

kernel_name = 'hybrid_dsa_s5_streaming_step'

F32 = jnp.float32


def rms_norm(x, g):
    xf = x.astype(F32)
    y = xf * lax.rsqrt(jnp.mean(xf * xf, axis=-1, keepdims=True) + NORM_EPS)
    return (y * g.astype(F32)).astype(x.dtype)


def split_cols(z):
    out, off = [], 0
    for w in SPLITS:
        out.append(z[..., off:off + w])
        off += w
    return out


def partial_rope(x, pos):
    half = ROT_DIM // 2
    inv = jnp.power(ROPE_THETA, -jnp.arange(half, dtype=F32) * (2.0 / ROT_DIM))
    ang = pos.astype(F32)[:, None] * inv[None, :]
    ang = ang.reshape((ang.shape[0],) + (1,) * (x.ndim - 3) + (half,))
    cos, sin = jnp.cos(ang), jnp.sin(ang)
    xf = x.astype(F32)
    x1, x2, rest = xf[..., :half], xf[..., half:ROT_DIM], xf[..., ROT_DIM:]
    out = jnp.concatenate([x1 * cos - x2 * sin, x2 * cos + x1 * sin, rest], axis=-1)
    return out.astype(x.dtype)


def dsa_attend(q, qi, wi, qpos, k_all, v_all, ki_all, kpos, n_sel):
    B, T = q.shape[0], q.shape[1]
    qb = QBLK if T % QBLK == 0 else T
    nblk = T // qb
    gather = jax.vmap(lambda a, i: a[i])
    kif = ki_all.astype(F32)
    kchunk = kpos // CHUNK

    def to_blocks(a):
        return jnp.swapaxes(a.reshape((B, nblk, qb) + a.shape[2:]), 0, 1)

    def one_block(blk):
        qb_, qib, wib, pb = blk
        s = jnp.einsum('bqhd,bsd->bqhs', qib.astype(F32), kif) * (IDX_DIM ** -0.5)
        score = jnp.einsum('bqh,bqhs->bqs', wib.astype(F32), jax.nn.relu(s))
        adm = kchunk[None, :] <= (pb // CHUNK)[:, None]
        score = jnp.where(adm[None], score, -jnp.inf)
        top_val, top_idx = lax.top_k(score, n_sel)
        valid = jnp.isfinite(top_val)
        k_sel = gather(k_all, top_idx).astype(F32)
        v_sel = gather(v_all, top_idx).astype(F32)
        qg = qb_.reshape(B, qb, N_KV_HEADS, N_HEADS // N_KV_HEADS, HEAD_DIM).astype(F32)
        logits = jnp.einsum('bqgrd,bqngd->bqgrn', qg, k_sel) * (HEAD_DIM ** -0.5)
        logits = jnp.where(valid[:, :, None, None, :], logits, -jnp.inf)
        p = jax.nn.softmax(logits, axis=-1)
        o = jnp.einsum('bqgrn,bqngd->bqgrd', p, v_sel)
        return o.reshape(B, qb, ATTN_WIDTH).astype(q.dtype)

    out = lax.map(one_block, (to_blocks(q), to_blocks(qi), to_blocks(wi), qpos.reshape(nblk, qb)))
    return jnp.swapaxes(out, 0, 1).reshape(B, T, ATTN_WIDTH)


def s5_discretize(lambda_re, lambda_im, log_dt, b_re, b_im):
    dt = jnp.exp(log_dt.astype(F32))[:, None]
    lr, li = lambda_re.astype(F32), lambda_im.astype(F32)
    mag = jnp.exp(lr * dt)
    ar, ai = mag * jnp.cos(li * dt), mag * jnp.sin(li * dt)
    den = lr * lr + li * li
    zr = ((ar - 1.0) * lr + ai * li) / den
    zi = (ai * lr - (ar - 1.0) * li) / den
    br, bi = b_re.astype(F32), b_im.astype(F32)
    bbar_re = zr[..., None] * br - zi[..., None] * bi
    bbar_im = zr[..., None] * bi + zi[..., None] * br
    return ar, ai, bbar_re, bbar_im


def complex_linear_combine(e1, e2):
    a1r, a1i, b1r, b1i = e1
    a2r, a2i, b2r, b2i = e2
    return (a2r * a1r - a2i * a1i, a2r * a1i + a2i * a1r,
            a2r * b1r - a2i * b1i + b2r, a2r * b1i + a2i * b1r + b2i)


def s5_mix(u, h0, lambda_re, lambda_im, log_dt, b_re, b_im, c_re, c_im, d_skip):
    B, T = u.shape[0], u.shape[1]
    ar, ai, bbr, bbi = s5_discretize(lambda_re, lambda_im, log_dt, b_re, b_im)
    ug = u.astype(F32).reshape(B, T, N_SSM_GROUPS, SSM_GROUP)
    bu_re = jnp.einsum('btgc,gpc->btgp', ug, bbr)
    bu_im = jnp.einsum('btgc,gpc->btgp', ug, bbi)
    if h0 is not None:
        h0r, h0i = h0[0].astype(F32), h0[1].astype(F32)
        bu_re = bu_re.at[:, 0].add(ar * h0r - ai * h0i)
        bu_im = bu_im.at[:, 0].add(ar * h0i + ai * h0r)
    a_re = jnp.broadcast_to(ar, bu_re.shape)
    a_im = jnp.broadcast_to(ai, bu_im.shape)
    _, _, h_re, h_im = lax.associative_scan(complex_linear_combine, (a_re, a_im, bu_re, bu_im), axis=1)
    y = (jnp.einsum('gcp,btgp->btgc', c_re.astype(F32), h_re)
         - jnp.einsum('gcp,btgp->btgc', c_im.astype(F32), h_im))
    y = y.reshape(B, T, SSM_WIDTH) + d_skip.astype(F32) * u.astype(F32)
    return y, h_re[:, -1], h_im[:, -1]


def hybrid_layer(x, c, pos, past, prm):
    (w_mod, b_mod, g_norm, w_in, lambda_re, lambda_im, log_dt, ssm_b_re, ssm_b_im,
     ssm_c_re, ssm_c_im, d_skip, w_glu, w_attn_proj, w_ssm_proj, w_out) = prm
    B, T, _ = x.shape
    mod = jax.nn.silu(c) @ w_mod + b_mod
    shift, scale, gate = mod[:, :D_MODEL], mod[:, D_MODEL:2 * D_MODEL], mod[:, 2 * D_MODEL:]
    h = rms_norm(x, g_norm) * (1.0 + scale[:, None, :]) + shift[:, None, :]
    z = h @ w_in
    zq, zk, zv, zqi, zki, zwi, zga, zu, zgs, zma, zmb = split_cols(z)
    q = partial_rope(zq.reshape(B, T, N_HEADS, HEAD_DIM), pos)
    k = partial_rope(zk.reshape(B, T, N_KV_HEADS, HEAD_DIM), pos)
    v = zv.reshape(B, T, N_KV_HEADS, HEAD_DIM)
    qi = partial_rope(zqi.reshape(B, T, N_IDX_HEADS, IDX_DIM), pos)
    ki = partial_rope(zki, pos)
    wi = zwi * (N_IDX_HEADS ** -0.5)
    if past is None:
        k_all, v_all, ki_all, kpos, h0 = k, v, ki, pos, None
    else:
        ck, cv, cki, h0r, h0i = past
        k_all = jnp.concatenate([ck.astype(k.dtype), k], axis=1)
        v_all = jnp.concatenate([cv.astype(v.dtype), v], axis=1)
        ki_all = jnp.concatenate([cki.astype(ki.dtype), ki], axis=1)
        kpos = jnp.arange(k_all.shape[1], dtype=jnp.int32)
        h0 = (h0r, h0i)
    n_sel = min(TOPK_MAX, k_all.shape[1] // 4)
    o_attn = dsa_attend(q, qi, wi, pos, k_all, v_all, ki_all, kpos, n_sel)
    branch_a = (o_attn * jax.nn.silu(zga)) @ w_attn_proj
    y_ssm, h_re, h_im = s5_mix(zu, h0, lambda_re, lambda_im, log_dt, ssm_b_re, ssm_b_im,
                               ssm_c_re, ssm_c_im, d_skip)
    g_lin = jax.nn.gelu(y_ssm).astype(x.dtype) @ w_glu
    y_glu = g_lin[..., :SSM_WIDTH] * jax.nn.sigmoid(g_lin[..., SSM_WIDTH:])
    branch_b = (y_glu.astype(x.dtype) * jax.nn.silu(zgs)) @ w_ssm_proj
    merged = jax.nn.sigmoid(zma) * branch_a + jax.nn.sigmoid(zmb) * branch_b
    x = x + gate[:, None, :] * (merged @ w_out)
    return x, k, v, ki, h_re, h_im


def setup_inputs(seed: int = 0) -> dict:
    key = jax.random.key(seed)
    ks = jax.random.split(key, 32)

    def nrm(k, shape, s):
        return jax.random.normal(k, shape, F32) * s

    G, P, C = N_SSM_GROUPS, SSM_STATE, SSM_GROUP
    return {
        'x_prompt': nrm(ks[0], (BATCH, SEQ, D_MODEL), 1.0),
        'x_sample': nrm(ks[1], (DEC_BATCH, DEC_SEQ, D_MODEL), 1.0),
        'cache_k': nrm(ks[2], (DEPTH, DEC_BATCH, PAST_LEN, N_KV_HEADS, HEAD_DIM), 1.0),
        'cache_v': nrm(ks[3], (DEPTH, DEC_BATCH, PAST_LEN, N_KV_HEADS, HEAD_DIM), 1.0),
        'cache_idx_k': nrm(ks[4], (DEPTH, DEC_BATCH, PAST_LEN, IDX_DIM), 1.0),
        'state_ssm_re': nrm(ks[5], (DEPTH, DEC_BATCH, G, P), 0.1),
        'state_ssm_im': nrm(ks[6], (DEPTH, DEC_BATCH, G, P), 0.1),
        'c_prompt': nrm(ks[7], (BATCH, D_MODEL), 1.0),
        'c_sample': nrm(ks[8], (DEC_BATCH, D_MODEL), 1.0),
        'w_mod': nrm(ks[9], (DEPTH, D_MODEL, 3 * D_MODEL), 0.5 * D_MODEL ** -0.5),
        'b_mod': nrm(ks[10], (DEPTH, 3 * D_MODEL), 0.02),
        'g_norm': 1.0 + nrm(ks[11], (DEPTH, D_MODEL), 0.02),
        'w_in': nrm(ks[12], (DEPTH, D_MODEL, IN_WIDTH), D_MODEL ** -0.5),
        'lambda_re': -0.5 + nrm(ks[13], (DEPTH, G, P), 0.01),
        'lambda_im': math.pi * jnp.arange(P, dtype=F32) + nrm(ks[14], (DEPTH, G, P), 0.01),
        'log_dt': jax.random.uniform(ks[15], (DEPTH, G), F32, math.log(1e-3), math.log(1e-1)),
        'ssm_b_re': nrm(ks[16], (DEPTH, G, P, C), (2 * C) ** -0.5),
        'ssm_b_im': nrm(ks[17], (DEPTH, G, P, C), (2 * C) ** -0.5),
        'ssm_c_re': nrm(ks[18], (DEPTH, G, C, P), P ** -0.5),
        'ssm_c_im': nrm(ks[19], (DEPTH, G, C, P), P ** -0.5),
        'd_skip': nrm(ks[20], (DEPTH, SSM_WIDTH), 1.0),
        'w_glu': nrm(ks[21], (DEPTH, SSM_WIDTH, 2 * SSM_WIDTH), SSM_WIDTH ** -0.5),
        'w_attn_proj': nrm(ks[22], (DEPTH, ATTN_WIDTH, D_MODEL), ATTN_WIDTH ** -0.5),
        'w_ssm_proj': nrm(ks[23], (DEPTH, SSM_WIDTH, D_MODEL), SSM_WIDTH ** -0.5),
        'w_out': nrm(ks[24], (DEPTH, D_MODEL, D_MODEL), D_MODEL ** -0.5),
        'g_final': 1.0 + nrm(ks[25], (D_MODEL,), 0.02),
    }


def reference(x_prompt, x_sample, cache_k, cache_v, cache_idx_k, state_ssm_re, state_ssm_im,
              c_prompt, c_sample, w_mod, b_mod, g_norm, w_in, lambda_re, lambda_im, log_dt,
              ssm_b_re, ssm_b_im, ssm_c_re, ssm_c_im, d_skip, w_glu, w_attn_proj, w_ssm_proj,
              w_out, g_final):
    pos_p = jnp.arange(x_prompt.shape[1], dtype=jnp.int32)
    past_len = cache_k.shape[2]
    pos_s = past_len + jnp.arange(x_sample.shape[1], dtype=jnp.int32)
    xp, xs = x_prompt, x_sample
    kp_l, vp_l, kip_l, hrp_l, hip_l = [], [], [], [], []
    ks_l, vs_l, kis_l, hrs_l, his_l = [], [], [], [], []
    for l in range(DEPTH):
        prm = (w_mod[l], b_mod[l], g_norm[l], w_in[l], lambda_re[l], lambda_im[l], log_dt[l],
               ssm_b_re[l], ssm_b_im[l], ssm_c_re[l], ssm_c_im[l], d_skip[l], w_glu[l],
               w_attn_proj[l], w_ssm_proj[l], w_out[l])
        xp, kp, vp, kip, hrp, hip = hybrid_layer(xp, c_prompt, pos_p, None, prm)
        past = (cache_k[l], cache_v[l], cache_idx_k[l], state_ssm_re[l], state_ssm_im[l])
        xs, ks_, vs_, kis, hrs, his = hybrid_layer(xs, c_sample, pos_s, past, prm)
        kp_l.append(kp); vp_l.append(vp); kip_l.append(kip)
        hrp_l.append(hrp.astype(state_ssm_re.dtype)); hip_l.append(hip.astype(state_ssm_im.dtype))
        ks_l.append(ks_); vs_l.append(vs_); kis_l.append(kis)
        hrs_l.append(hrs.astype(state_ssm_re.dtype)); his_l.append(his.astype(state_ssm_im.dtype))
    y_prompt = rms_norm(xp, g_final)
    y_sample = rms_norm(xs, g_final)
    return (y_prompt, y_sample,
            jnp.stack(kp_l), jnp.stack(vp_l), jnp.stack(kip_l), jnp.stack(hrp_l), jnp.stack(hip_l),
            jnp.stack(ks_l), jnp.stack(vs_l), jnp.stack(kis_l), jnp.stack(hrs_l), jnp.stack(his_l))
```

```python
import math
from contextlib import ExitStack

import numpy as np
import concourse.bass as bass
import concourse.mybir as mybir
from concourse.bass_utils import run_bass_kernel_spmd

F32 = mybir.dt.float32
BF16 = mybir.dt.bfloat16
I32 = mybir.dt.int32
ALU = mybir.AluOpType
AF = mybir.ActivationFunctionType
AX = mybir.AxisListType

D = 1024
NCORE = 8
TP_ = 2048
TS_ = 64
PAST = 1024
INW = 4936
TOKW = 1352
NPIECE = 14
NITER = 20
NEG = -30000.0
TWO_PI = 2.0 * math.pi


class Buf:
    def __init__(self, t, name):
        self.t = t
        self.name = name
        self.w = None
        self.r = {}

    def __getitem__(self, idx):
        return self.t[idx]


class Tok(Buf):
    def __init__(self, name):
        Buf.__init__(self, None, name)


class FW:
    def __init__(self, nc, es):
        self.nc = nc
        self.es = es
        self.eng = {'pe': nc.tensor, 'dve': nc.vector, 'act': nc.scalar, 'pool': nc.gpsimd, 'sp': nc.sync}
        self.sem = {k: es.enter_context(nc.semaphore('s_' + k)) for k in self.eng}
        self.cnt = {k: 0 for k in self.eng}
        self.seen = {k: {} for k in self.eng}
        self.ND = 24
        self.dsem = [es.enter_context(nc.semaphore('d%d' % i)) for i in range(self.ND)]
        self.dcnt = [0] * self.ND
        self.dnext = 0
        self.semobj = dict(self.sem)
        for i, s in enumerate(self.dsem):
            self.semobj['d%d' % i] = s
        self.ninstr = 0

    def sb(self, name, shape, dt, es=None):
        return Buf((es or self.es).enter_context(self.nc.sbuf_tensor(name, shape, dt)), name)

    def ps(self, name, shape, dt):
        return Buf(self.es.enter_context(self.nc.psum_tensor(name, shape, dt)), name)

    def _wait(self, e, key, val):
        if val is None or val <= 0:
            return
        if e == 'pe' and key == 'pe' and not getattr(self, 'pe_serial', False):
            return
        if self.seen[e].get(key, 0) >= val:
            return
        self.eng[e].wait_ge(self.semobj[key], val)
        self.seen[e][key] = val

    def deps(self, e, reads, writes):
        for b in reads:
            if b.w is not None:
                self._wait(e, *b.w)
        for b in writes:
            if b.w is not None:
                self._wait(e, *b.w)
            for k, v in b.r.items():
                self._wait(e, k, v)

    def mark(self, tick, reads, writes):
        for b in reads:
            b.r[tick[0]] = max(b.r.get(tick[0], 0), tick[1])
        for b in writes:
            b.w = tick
            b.r = {}

    def op(self, e, fn, reads=(), writes=(), inc=True):
        if getattr(self, 'pe_serial', False):
            inc = True
        self.deps(e, reads, writes)
        ins = fn(self.eng[e])
        self.ninstr += 1
        if inc:
            ins.then_inc(self.sem[e], 1)
            self.cnt[e] += 1
        tick = (e, self.cnt[e] if inc else self.cnt[e] + 1)
        self.mark(tick, reads, writes)
        return tick

    def dma(self, e, out, in_, reads=(), writes=(), **kw):
        i = self.dnext
        self.dnext = (self.dnext + 1) % self.ND
        key = 'd%d' % i
        self._wait(e, key, self.dcnt[i])
        self.deps(e, reads, writes)
        self.eng[e].dma_start(out=out, in_=in_, **kw).then_inc(self.dsem[i], 16)
        self.ninstr += 1
        self.dcnt[i] += 16
        tick = (key, self.dcnt[i])
        self.mark(tick, reads, writes)
        return tick

    def barrier(self):
        for e in self.eng:
            for k in self.eng:
                self._wait(e, k, self.cnt[k])
            for i in range(self.ND):
                self._wait(e, 'd%d' % i, self.dcnt[i])


class _Stop(Exception):
    pass


def build_nc(stage=99, sub=99):
    nc = bass.Bass("TRN2", target_bir_lowering=False)

    def din(name, shape, dt=F32):
        return nc.dram_tensor(name, list(shape), dt, kind="ExternalInput").ap()

    def dout(name, shape, dt=F32):
        return nc.dram_tensor(name, list(shape), dt, kind="ExternalOutput").ap()

    xp = din("xp", [2, TP_, D]); xs = din("xs", [4, TS_, D])
    ck = din("ck", [4, PAST, 128]); cv = din("cv", [4, PAST, 128]); cki = din("cki", [4, PAST, 64])
    h0r = din("h0r", [4, 2048]); h0i = din("h0i", [4, 2048])
    call = din("call", [6, D])
    w_mod = din("w_mod", [D, 3 * D]); b_mod = din("b_mod", [1, 3 * D]); g_norm = din("g_norm", [1, D])
    w_in = din("w_in", [D, INW])
    lam_re = din("lam_re", [1, 2048]); lam_im = din("lam_im", [1, 2048]); log_dt = din("log_dt", [1, 32])
    b_re = din("b_re", [2048, 16]); b_im = din("b_im", [2048, 16])
    c_re = din("c_re", [512, 64]); c_im = din("c_im", [512, 64])
    d_skip = din("d_skip", [1, 512])
    w_glu = din("w_glu", [512, D]); w_ap = din("w_ap", [512, D]); w_sp = din("w_sp", [512, D])
    w_out = din("w_out", [D, D]); g_final = din("g_final", [1, D])

    yp = dout("yp", [2, TP_, D]); ys = dout("ys", [4, TS_, D])
    kp = dout("kp", [2, TP_, 128]); vp = dout("vp", [2, TP_, 128]); kip = dout("kip", [2, TP_, 64])
    hrp = dout("hrp", [2, 2048]); hip = dout("hip", [2, 2048])
    ksm = dout("ksm", [4, TS_, 128]); vsm = dout("vsm", [4, TS_, 128]); kism = dout("kism", [4, TS_, 64])
    hrs = dout("hrs", [4, 2048]); his = dout("his", [4, 2048])

    modscr = nc.dram_tensor("modscr", [6, 3 * D], F32, kind="Internal").ap()
    wsc = nc.dram_tensor("wsc", [NPIECE, 128, 8 * 256], BF16, kind="Internal").ap()

    es = ExitStack()
    with es:
        nc_ctx = es.enter_context(nc.allow_non_contiguous_dma(reason="small strided setup loads"))
        fw = FW(nc, es)
        op = fw.op
        dma = fw.dma

        win = fw.sb("win", [128, 8, TOKW], BF16)
        wglu = fw.sb("wglu", [128, 4, D], BF16)
        wap = fw.sb("wap", [128, 4, D], BF16)
        wsp = fw.sb("wsp", [128, 4, D], BF16)
        wout = fw.sb("wout", [128, 8, D], BF16)
        wring = [fw.sb("wring%d" % i, [128, 8, 256], BF16) for i in range(2)]
        wbz = fw.sb("wbz", [128, 4, 4, 2, 128], BF16)
        cz = fw.sb("cz", [128, 4, 4, 2, 128], BF16)
        cosT = fw.sb("cosT", [128, 16, 64], F32)
        sinT = fw.sb("sinT", [128, 16, 64], F32)
        rho = fw.sb("rho", [128, 16], F32)
        dsk = fw.sb("dsk", [128, 4], F32)
        kT = fw.sb("kT", [128, 2176], BF16)
        kiT = fw.sb("kiT", [128, 2176], BF16)
        kT_tok = [Tok("kTt%d" % i) for i in range(17)]
        kiT_tok = [Tok("kiTt%d" % i) for i in range(17)]
        vaug = [fw.sb("vaug%d" % i, [128, 2, 65], BF16) for i in range(17)]
        idf = fw.sb("idf", [128, 128], F32)
        idb = fw.sb("idb", [128, 128], BF16)
        id4 = fw.sb("id4", [128, 4, 128], BF16)
        zb = fw.sb("zb", [128, 512], BF16)
        ropeC = fw.sb("ropeC", [128, 17, 8], F32)
        ropeS = fw.sb("ropeS", [128, 17, 8], F32)
        Amod = fw.sb("Amod", [128, 6, 8], F32)
        Bmod = fw.sb("Bmod", [128, 6, 8], F32)
        gfB = fw.sb("gfB", [128, D], F32)
        gateB = fw.sb("gateB", [128, D], F32)
        hprev = fw.sb("hprev", [128, 2, 16], F32)
        hp_tok = [Tok("hp%d" % i) for i in range(4)]

        PA = [fw.ps("psA%d" % i, [128, 512], F32) for i in range(4)]
        PB = [fw.ps("psB%d" % i, [128, 512], F32) for i in range(2)]
        PC = [fw.ps("psC%d" % i, [128, 512], F32) for i in range(2)]
        ringA = [0]

        def nextA():
            b = PA[ringA[0] % 4]
            ringA[0] += 1
            return b
        ringB = [0]

        def nextB():
            b = PB[ringB[0] % 2]
            ringB[0] += 1
            return b

        op('pool', lambda g: g.memset(idf[:], 0.0), writes=[idf])
        op('pool', lambda g: g.affine_select(out=idf[:], in_=idf[:], pattern=[[-1, 128]], compare_op=ALU.not_equal,
                                              fill=1.0, base=0, channel_multiplier=1), reads=[idf], writes=[idf])
        op('dve', lambda v: v.tensor_copy(out=idb[:], in_=idf[:]), reads=[idf], writes=[idb])
        for j in range(4):
            op('dve', lambda v: v.tensor_copy(out=id4[:, j, :], in_=idf[:]), reads=[idf], writes=[id4])
        op('pool', lambda g: g.memset(zb[:], 0.0), writes=[zb])
        for i in range(17):
            op('pool', lambda g: g.memset(vaug[i][:], 1.0), writes=[vaug[i]])
        if stage <= 0:
            fw.barrier()
            return nc

        ses = ExitStack()
        with ses:
            NSTG = 3
            stg = [fw.sb("stg%d" % i, [128, 3072], F32, es=ses) for i in range(NSTG)]
            stgi = [0]

            def load_cast(dst_ap, dst_buf, src_ap, ncols):
                s = stg[stgi[0] % NSTG]
                ce = ['dve', 'pool', 'act'][stgi[0] % 3]
                stgi[0] += 1
                dma('sp' if stgi[0] % 2 == 0 else 'act', s[:, 0:ncols], src_ap, writes=[s])
                if ce == 'act':
                    op('act', lambda a: a.activation(out=dst_ap, in_=s[:, 0:ncols], func=AF.Copy), reads=[s], writes=[dst_buf])
                else:
                    op(ce, lambda v: v.tensor_copy(out=dst_ap, in_=s[:, 0:ncols]), reads=[s], writes=[dst_buf])

            cT = fw.sb("cT", [128, 8, 6], F32, es=ses)
            for s_ in range(6):
                dma('sp', cT[:, :, s_], call[s_, :].rearrange("(k p) -> p k", p=128), writes=[cT])
            sgc = fw.sb("sgc", [128, 8, 6], F32, es=ses)
            op('act', lambda a: a.activation(out=sgc[:], in_=cT[:], func=AF.Sigmoid), reads=[cT], writes=[sgc])
            op('dve', lambda v: v.tensor_tensor(out=cT[:], in0=cT[:], in1=sgc[:], op=ALU.mult), reads=[cT, sgc], writes=[cT])
            modps = PA + PB
            for k in range(8):
                s = stg[stgi[0] % NSTG]
                stgi[0] += 1
                dma('sp' if k % 2 == 0 else 'act', s[:, :], w_mod[k * 128:(k + 1) * 128, :], writes=[s])
                for blk in range(6):
                    op('pe', lambda t: t.matmul(modps[blk][0:6, :], lhsT=cT[:, k, :], rhs=s[:, blk * 512:(blk + 1) * 512],
                                                 start=(k == 0), stop=(k == 7)), reads=[cT, s], writes=[modps[blk]], inc=(k == 7 or blk == 5))
            modrow = fw.sb("modrow", [6, 3 * D], F32, es=ses)
            dma('sp', modrow[:, :], b_mod[0:1, :].to_broadcast([6, 3 * D]), writes=[modrow])
            for blk in range(6):
                op('dve', lambda v: v.tensor_tensor(out=modrow[:, blk * 512:(blk + 1) * 512], in0=modps[blk][0:6, :],
                                                     in1=modrow[:, blk * 512:(blk + 1) * 512], op=ALU.add),
                   reads=[modps[blk], modrow], writes=[modrow])
            scr_tok = Tok("modscr")
            dma('sp', modscr[:, :], modrow[:, :], reads=[modrow], writes=[scr_tok])
            modT = fw.sb("modT", [128, 6, 24], F32, es=ses)
            for s_ in range(6):
                dma('sp', modT[:, s_, :], modscr[s_, :].rearrange("(j p) -> p j", p=128), reads=[scr_tok], writes=[modT])
            gnT = fw.sb("gnT", [128, 8], F32, es=ses)
            dma('sp', gnT[:, :], g_norm[0, :].rearrange("(j p) -> p j", p=128), writes=[gnT])
            for s_ in range(6):
                op('dve', lambda v: v.scalar_tensor_tensor(out=Amod[:, s_, :], in0=modT[:, s_, 8:16], scalar=1.0, in1=gnT[:, :],
                                                            op0=ALU.add, op1=ALU.mult), reads=[modT, gnT], writes=[Amod])
                op('dve', lambda v: v.tensor_copy(out=Bmod[:, s_, :], in_=modT[:, s_, 0:8]), reads=[modT], writes=[Bmod])
            dma('sp', gfB[:, :], g_final[0:1, :].to_broadcast([128, D]), writes=[gfB])
            if stage <= 1:
                fw.barrier()
                return nc

            for k in range(8):
                load_cast(win[:, k, :], win, w_in[k * 128:(k + 1) * 128, 0:TOKW], TOKW)
            wtok = Tok("wsc")
            wtmp = [fw.sb("wtmp%d" % i, [128, 1792], BF16, es=ses) for i in range(4)]
            for k in range(8):
                for half in range(2):
                    c0 = TOKW + half * 1792
                    t_ = wtmp[(2 * k + half) % 4]
                    load_cast(t_[:, 0:1792], t_, w_in[k * 128:(k + 1) * 128, c0:c0 + 1792], 1792)
                    dma('act', wsc[half * 7:(half + 1) * 7, :, k * 256:(k + 1) * 256].rearrange("a p c -> p a c"),
                        t_[:, 0:1792].rearrange("p (a c) -> p a c", c=256), reads=[t_], writes=[wtok])
            for (wsb, wsrc, nk) in ((wglu, w_glu, 4), (wap, w_ap, 4), (wsp, w_sp, 4), (wout, w_out, 8)):
                for k in range(nk):
                    load_cast(wsb[:, k, :], wsb, wsrc[k * 128:(k + 1) * 128, :], D)
            if stage <= 2:
                fw.barrier()
                return nc

            posf = fw.sb("posf", [128, 17], F32, es=ses)
            op('pool', lambda g: g.iota(posf[:, 0:16], pattern=[[128, 16]], base=0, channel_multiplier=1,
                                        allow_small_or_imprecise_dtypes=True), writes=[posf])
            op('pool', lambda g: g.iota(posf[:, 16:17], pattern=[[0, 1]], base=PAST, channel_multiplier=1,
                                        allow_small_or_imprecise_dtypes=True), reads=[posf], writes=[posf])
            ang = fw.sb("ang", [128, 17, 8], F32, es=ses)
            for j in range(8):
                invj = float(500000.0 ** (-j * 2.0 / 16.0))
                op('dve', lambda v: v.tensor_scalar(out=ang[:, :, j], in0=posf[:, :], scalar1=invj, scalar2=None, op0=ALU.mult),
                   reads=[posf], writes=[ang])
            rr = fw.sb("rr", [128, 144], F32, es=ses)
            ri_ = fw.sb("ri_", [128, 144], I32, es=ses)
            rm = fw.sb("rm", [128, 144], F32, es=ses)

            def sin_of(dst_ap, dst_buf, src_ap, src_buf, n, shift):
                r = rr[:, 0:n]
                op('dve', lambda v: v.tensor_scalar(out=r, in0=src_ap, scalar1=shift, scalar2=1.0 / TWO_PI, op0=ALU.add, op1=ALU.mult),
                   reads=[src_buf], writes=[rr])
                op('dve', lambda v: v.tensor_copy(out=ri_[:, 0:n], in_=r), reads=[rr], writes=[ri_])
                op('dve', lambda v: v.tensor_copy(out=rm[:, 0:n], in_=ri_[:, 0:n]), reads=[ri_], writes=[rm])
                op('dve', lambda v: v.tensor_tensor(out=r, in0=r, in1=rm[:, 0:n], op=ALU.subtract), reads=[rr, rm], writes=[rr])
                op('dve', lambda v: v.tensor_scalar(out=rm[:, 0:n], in0=r, scalar1=0.5, scalar2=None, op0=ALU.is_gt), reads=[rr], writes=[rm])
                op('dve', lambda v: v.tensor_tensor(out=r, in0=r, in1=rm[:, 0:n], op=ALU.subtract), reads=[rr, rm], writes=[rr])
                op('dve', lambda v: v.tensor_scalar(out=rm[:, 0:n], in0=r, scalar1=-0.5, scalar2=None, op0=ALU.is_lt), reads=[rr], writes=[rm])
                op('dve', lambda v: v.tensor_tensor(out=r, in0=r, in1=rm[:, 0:n], op=ALU.add), reads=[rr, rm], writes=[rr])
                op('dve', lambda v: v.tensor_scalar(out=r, in0=r, scalar1=0.4999999, scalar2=-0.4999999, op0=ALU.min, op1=ALU.max),
                   reads=[rr], writes=[rr])
                op('act', lambda a: a.activation(out=dst_ap, in_=r, func=AF.Sin, scale=TWO_PI), reads=[rr], writes=[dst_buf])

            sin_of(ropeS[:].rearrange("p a b -> p (a b)"), ropeS, ang[:].rearrange("p a b -> p (a b)"), ang, 136, 0.0)
            sin_of(ropeC[:].rearrange("p a b -> p (a b)"), ropeC, ang[:].rearrange("p a b -> p (a b)"), ang, 136, math.pi / 2)
            if stage <= 3:
                fw.barrier()
                return nc

            lr = fw.sb("lr", [128, 16], F32, es=ses); li = fw.sb("li", [128, 16], F32, es=ses)
            dtn = fw.sb("dtn", [128, 16], F32, es=ses)
            dma('sp', lr[:, :], lam_re[0, :].rearrange("(j p) -> p j", p=128), writes=[lr])
            dma('sp', li[:, :], lam_im[0, :].rearrange("(j p) -> p j", p=128), writes=[li])
            ldv = log_dt[0, :].rearrange("(j two) -> two j", two=2)
            dma('sp', dtn[0:64, :], ldv[0:1, :].to_broadcast([64, 16]), writes=[dtn])
            dma('sp', dtn[64:128, :], ldv[1:2, :].to_broadcast([64, 16]), writes=[dtn])
            op('act', lambda a: a.activation(out=dtn[:], in_=dtn[:], func=AF.Exp), reads=[dtn], writes=[dtn])
            angs = fw.sb("angs", [128, 16], F32, es=ses)
            op('dve', lambda v: v.tensor_tensor(out=angs[:], in0=li[:], in1=dtn[:], op=ALU.mult), reads=[li, dtn], writes=[angs])
            op('dve', lambda v: v.tensor_tensor(out=rho[:], in0=lr[:], in1=dtn[:], op=ALU.mult), reads=[lr, dtn], writes=[rho])
            op('act', lambda a: a.activation(out=rho[:], in_=rho[:], func=AF.Exp), reads=[rho], writes=[rho])
            c1 = fw.sb("c1", [128, 16], F32, es=ses); s1 = fw.sb("s1", [128, 16], F32, es=ses)
            sin_of(s1[:], s1, angs[:], angs, 16, 0.0)
            sin_of(c1[:], c1, angs[:], angs, 16, math.pi / 2)
            ar = fw.sb("ar", [128, 16], F32, es=ses); ai = fw.sb("ai", [128, 16], F32, es=ses)
            op('dve', lambda v: v.tensor_tensor(out=ar[:], in0=rho[:], in1=c1[:], op=ALU.mult), reads=[rho, c1], writes=[ar])
            op('dve', lambda v: v.tensor_tensor(out=ai[:], in0=rho[:], in1=s1[:], op=ALU.mult), reads=[rho, s1], writes=[ai])
            den = fw.sb("den", [128, 16], F32, es=ses); t1 = fw.sb("t1", [128, 16], F32, es=ses); t2 = fw.sb("t2", [128, 16], F32, es=ses)
            zr = fw.sb("zr", [128, 16], F32, es=ses); zi = fw.sb("zi", [128, 16], F32, es=ses); am1 = fw.sb("am1", [128, 16], F32, es=ses)
            op('dve', lambda v: v.tensor_tensor(out=den[:], in0=lr[:], in1=lr[:], op=ALU.mult), reads=[lr], writes=[den])
            op('dve', lambda v: v.tensor_tensor(out=t1[:], in0=li[:], in1=li[:], op=ALU.mult), reads=[li], writes=[t1])
            op('dve', lambda v: v.tensor_tensor(out=den[:], in0=den[:], in1=t1[:], op=ALU.add), reads=[den, t1], writes=[den])
            op('dve', lambda v: v.reciprocal(out=den[:], in_=den[:]), reads=[den], writes=[den])
            op('dve', lambda v: v.tensor_scalar(out=am1[:], in0=ar[:], scalar1=-1.0, scalar2=None, op0=ALU.add), reads=[ar], writes=[am1])
            op('dve', lambda v: v.tensor_tensor(out=t1[:], in0=am1[:], in1=lr[:], op=ALU.mult), reads=[am1, lr], writes=[t1])
            op('dve', lambda v: v.tensor_tensor(out=t2[:], in0=ai[:], in1=li[:], op=ALU.mult), reads=[ai, li], writes=[t2])
            op('dve', lambda v: v.tensor_tensor(out=t1[:], in0=t1[:], in1=t2[:], op=ALU.add), reads=[t1, t2], writes=[t1])
            op('dve', lambda v: v.tensor_tensor(out=zr[:], in0=t1[:], in1=den[:], op=ALU.mult), reads=[t1, den], writes=[zr])
            op('dve', lambda v: v.tensor_tensor(out=t1[:], in0=ai[:], in1=lr[:], op=ALU.mult), reads=[ai, lr], writes=[t1])
            op('dve', lambda v: v.tensor_tensor(out=t2[:], in0=am1[:], in1=li[:], op=ALU.mult), reads=[am1, li], writes=[t2])
            op('dve', lambda v: v.tensor_tensor(out=t1[:], in0=t1[:], in1=t2[:], op=ALU.subtract), reads=[t1, t2], writes=[t1])
            op('dve', lambda v: v.tensor_tensor(out=zi[:], in0=t1[:], in1=den[:], op=ALU.mult), reads=[t1, den], writes=[zi])

            op('dve', lambda v: v.tensor_copy(out=cosT[:, :, 0], in_=c1[:]), reads=[c1], writes=[cosT])
            op('dve', lambda v: v.tensor_copy(out=sinT[:, :, 0], in_=s1[:]), reads=[s1], writes=[sinT])
            tA = fw.sb("tA", [128, 16, 32], F32, es=ses); tB = fw.sb("tB", [128, 16, 32], F32, es=ses)
            L = 1
            while L < 64:
                cl = cosT[:, :, L - 1:L].to_broadcast([128, 16, L])
                sl = sinT[:, :, L - 1:L].to_broadcast([128, 16, L])
                op('dve', lambda v: v.tensor_tensor(out=tA[:, :, 0:L], in0=cosT[:, :, 0:L], in1=cl, op=ALU.mult), reads=[cosT], writes=[tA])
                op('dve', lambda v: v.tensor_tensor(out=tB[:, :, 0:L], in0=sinT[:, :, 0:L], in1=sl, op=ALU.mult), reads=[sinT], writes=[tB])
                op('dve', lambda v: v.tensor_tensor(out=tA[:, :, 0:L], in0=tA[:, :, 0:L], in1=tB[:, :, 0:L], op=ALU.subtract), reads=[tA, tB], writes=[tA])
                op('dve', lambda v: v.tensor_tensor(out=tB[:, :, 0:L], in0=sinT[:, :, 0:L], in1=cl, op=ALU.mult), reads=[sinT, cosT], writes=[tB])
                op('dve', lambda v: v.tensor_copy(out=cosT[:, :, L:2 * L], in_=tA[:, :, 0:L]), reads=[tA], writes=[cosT])
                op('dve', lambda v: v.tensor_tensor(out=tA[:, :, 0:L], in0=cosT[:, :, 0:L], in1=sl, op=ALU.mult), reads=[cosT, sinT], writes=[tA])
                op('dve', lambda v: v.tensor_tensor(out=sinT[:, :, L:2 * L], in0=tA[:, :, 0:L], in1=tB[:, :, 0:L], op=ALU.add), reads=[tA, tB], writes=[sinT])
                L *= 2

            bn_r = fw.sb("bn_r", [128, 16, 16], F32, es=ses); bn_i = fw.sb("bn_i", [128, 16, 16], F32, es=ses)
            dma('sp', bn_r[:], b_re.rearrange("(j p) c -> p j c", p=128), writes=[bn_r])
            dma('sp', bn_i[:], b_im.rearrange("(j p) c -> p j c", p=128), writes=[bn_i])
            bb_r = fw.sb("bb_r", [128, 16, 16], F32, es=ses); bb_i = fw.sb("bb_i", [128, 16, 16], F32, es=ses)
            tq = fw.sb("tq", [128, 16, 16], F32, es=ses)
            zrb = zr[:, :].unsqueeze(2).to_broadcast([128, 16, 16])
            zib = zi[:, :].unsqueeze(2).to_broadcast([128, 16, 16])
            op('dve', lambda v: v.tensor_tensor(out=bb_r[:], in0=bn_r[:], in1=zrb, op=ALU.mult), reads=[bn_r, zr], writes=[bb_r])
            op('dve', lambda v: v.tensor_tensor(out=tq[:], in0=bn_i[:], in1=zib, op=ALU.mult), reads=[bn_i, zi], writes=[tq])
            op('dve', lambda v: v.tensor_tensor(out=bb_r[:], in0=bb_r[:], in1=tq[:], op=ALU.subtract), reads=[bb_r, tq], writes=[bb_r])
            op('dve', lambda v: v.tensor_tensor(out=bb_i[:], in0=bn_i[:], in1=zrb, op=ALU.mult), reads=[bn_i, zr], writes=[bb_i])
            op('dve', lambda v: v.tensor_tensor(out=tq[:], in0=bn_r[:], in1=zib, op=ALU.mult), reads=[bn_r, zi], writes=[tq])
            op('dve', lambda v: v.tensor_tensor(out=bb_i[:], in0=bb_i[:], in1=tq[:], op=ALU.add), reads=[bb_i, tq], writes=[bb_i])
            bexp = [fw.sb("bexp%d" % i, [128, 128], F32, es=ses) for i in range(2)]
            cnat_r = fw.sb("cnat_r", [128, 4, 64], F32, es=ses); cnat_i = fw.sb("cnat_i", [128, 4, 64], F32, es=ses)
            dma('sp', cnat_r[:], c_re.rearrange("(j p) n -> p j n", p=128), writes=[cnat_r])
            dma('sp', cnat_i[:], c_im.rearrange("(j p) n -> p j n", p=128), writes=[cnat_i])
            pid = fw.sb("pid", [128, 1], I32, es=ses)
            op('pool', lambda g: g.iota(pid[:, :], pattern=[[0, 1]], base=0, channel_multiplier=1), writes=[pid])
            op('dve', lambda v: v.tensor_scalar(out=pid[:], in0=pid[:], scalar1=4, scalar2=1, op0=ALU.arith_shift_right, op1=ALU.bitwise_and),
               reads=[pid], writes=[pid])
            mb = fw.sb("mb", [128, 2], F32, es=ses)
            op('dve', lambda v: v.tensor_copy(out=mb[:, 1:2], in_=pid[:]), reads=[pid], writes=[mb])
            op('dve', lambda v: v.tensor_scalar(out=mb[:, 0:1], in0=mb[:, 1:2], scalar1=-1.0, scalar2=1.0, op0=ALU.mult, op1=ALU.add),
               reads=[mb], writes=[mb])
            xi_ = [0]
            for pc in range(4):
                for q in range(4):
                    pt = 4 * pc + q
                    for ri, bb in enumerate((bb_r, bb_i)):
                        be = bexp[xi_[0] % 2]
                        xi_[0] += 1
                        op('pool', lambda g: g.memset(be[:], 0.0), writes=[be])
                        op('dve', lambda v: v.tensor_copy(out=be[0:64, 32 * q:32 * q + 16], in_=bb[0:64, pt, :]), reads=[bb], writes=[be])
                        op('dve', lambda v: v.tensor_copy(out=be[64:128, 32 * q + 16:32 * q + 32], in_=bb[64:128, pt, :]), reads=[bb, be], writes=[be])
                        pb_ = nextB()
                        op('pe', lambda t: t.transpose(out=pb_[:, 0:128], in_=be[:], identity=idf[:]), reads=[be, idf], writes=[pb_])
                        op('act', lambda a: a.activation(out=wbz[:, pc, q, ri, :], in_=pb_[:, 0:128], func=AF.Copy), reads=[pb_], writes=[wbz])
                    for ri, cn in enumerate((cnat_r, cnat_i)):
                        be = bexp[xi_[0] % 2]
                        xi_[0] += 1
                        op('pool', lambda g: g.memset(be[:], 0.0), writes=[be])
                        sl_ = slice(32 * q, 32 * q + 32)
                        sgn = 1.0 if ri == 0 else -1.0
                        op('dve', lambda v: v.tensor_scalar(out=be[sl_, 0:64], in0=cn[sl_, pc, :], scalar1=mb[sl_, 0:1], scalar2=sgn,
                                                             op0=ALU.mult, op1=ALU.mult), reads=[cn, mb], writes=[be])
                        op('dve', lambda v: v.tensor_scalar(out=be[sl_, 64:128], in0=cn[sl_, pc, :], scalar1=mb[sl_, 1:2], scalar2=sgn,
                                                             op0=ALU.mult, op1=ALU.mult), reads=[cn, mb, be], writes=[be])
                        pb_ = nextB()
                        op('pe', lambda t: t.transpose(out=pb_[:, 0:128], in_=be[:], identity=idf[:]), reads=[be, idf], writes=[pb_])
                        op('act', lambda a: a.activation(out=cz[:, pc, q, ri, :], in_=pb_[:, 0:128], func=AF.Copy), reads=[pb_], writes=[cz])
            dma('sp', dsk[:, :], d_skip[0, :].rearrange("(j p) -> p j", p=128), writes=[dsk])
            fw.barrier()
            if stage <= 4:
                return nc
        xbuf = [fw.sb("xbuf%d" % i, [128, D], F32) for i in range(1)]
        hT = fw.sb("hT", [128, 8, 128], BF16)
        tokm = fw.sb("tokm", [128, TOKW], F32)
        rt = [fw.sb("rt%d" % i, [128, 19, 8], F32) for i in range(4)]
        qb = fw.sb("qb", [128, 512], BF16); qib = fw.sb("qib", [128, 512], BF16)
        kb2 = fw.sb("kb2", [128, 128], BF16); kib = fw.sb("kib", [128, 2, 64], BF16)
        qT2 = [fw.sb("qT%d" % i, [128, 4, 128], BF16) for i in range(2)]; qiT2 = [fw.sb("qiT%d" % i, [128, 4, 128], BF16) for i in range(2)]
        Dm2 = [fw.sb("Dm%d" % i, [128, 8, 128], BF16) for i in range(2)]
        aw2 = [fw.sb("aw%d" % i, [128, 8], F32) for i in range(2)]; sg82 = [fw.sb("sg8%d" % i, [128, 8], F32) for i in range(2)]
        relu = [fw.sb("relu%d" % i, [128, 512], BF16) for i in range(2)]
        score = fw.sb("score", [128, 2176], F32)
        negm = fw.sb("negm", [128, 2176], BF16)
        pT = [fw.sb("pT%d" % i, [128, 4, 128], BF16) for i in range(3)]
        osb = fw.sb("osb", [128, 8, 65], F32); rden = fw.sb("rden", [128, 8], F32)
        og = fw.sb("og", [128, 512], BF16)
        ogT = fw.sb("ogT", [128, 4, 128], BF16)
        sgaT2 = [fw.sb("sgaT%d" % i, [128, 4, 128], BF16) for i in range(2)]
        uf = fw.sb("uf", [128, 4, 128], F32); ub = fw.sb("ub", [128, 4, 128], BF16)
        sgs2 = [fw.sb("sgs%d" % i, [128, 4, 128], BF16) for i in range(2)]
        sma2 = [fw.sb("sma%d" % i, [128, 8, 128], BF16) for i in range(2)]; smb2 = [fw.sb("smb%d" % i, [128, 8, 128], BF16) for i in range(2)]
        mm = [fw.sb("mm%d" % i, [128, 4, 64], F32) for i in range(2)]
        xr = fw.sb("xr", [128, 4, 64], F32); xi = fw.sb("xi", [128, 4, 64], F32)
        gr2 = [fw.sb("gr%d" % i, [128, 4, 64], F32) for i in range(2)]; gi2 = [fw.sb("gi%d" % i, [128, 4, 64], F32) for i in range(2)]
        hrb = [fw.sb("hrb%d" % i, [128, 4, 64], BF16) for i in range(2)]
        hib = [fw.sb("hib%d" % i, [128, 4, 64], BF16) for i in range(2)]
        cty = [fw.sb("cty%d" % i, [128, 4], F32) for i in range(6)]
        ysum = fw.sb("ysum", [128, 128], F32); yt1 = fw.sb("yt1", [128, 128], F32); yt2 = fw.sb("yt2", [128, 128], F32)
        ygb2 = [fw.sb("ygb%d" % i, [128, 4, 128], BF16) for i in range(2)]
        sgl = fw.sb("sgl", [128, 128], F32); gt1 = fw.sb("gt1", [128, 128], F32)
        yglu = fw.sb("yglu", [128, 4, 128], BF16)
        bat = fw.sb("bat", [128, 128], F32); mgt = fw.sb("mgt", [128, 128], F32)
        merged = fw.sb("merged", [128, 8, 128], BF16)
        res = fw.sb("res", [128, D], F32); ftmp = fw.sb("ftmp", [128, D], F32)
        st = [fw.sb("st%d" % i, [128, 1], F32) for i in range(4)]
        blo = fw.sb("blo", [128, 1], F32); bmid = fw.sb("bmid", [128, 1], F32); bcnt = fw.sb("bcnt", [128, 1], F32)
        bhalf = fw.sb("bhalf", [128, NITER + 1], F32); btmp = fw.sb("btmp", [128, 1], F32)
        btab = fw.sb("btab", [128, NITER + 1], F32)
        one_c = fw.sb("one_c", [128, 3], F32)
        for ci_, cv_ in enumerate((1.0, 40.0, 63.830764864229232)):
            op('pool', lambda g: g.memset(one_c[:, ci_:ci_ + 1], cv_), reads=[one_c], writes=[one_c])
        for it in range(NITER + 1):
            op('pool', lambda g: g.memset(btab[:, it:it + 1], float(0.5 ** (it + 1))), reads=[btab], writes=[btab])
        cst = fw.sb("cst", [128, 128], F32)
        wri = [0]

        class WView(Buf):
            def __init__(self, base):
                self.base = base
                self.ap3 = base.t[:].bitcast(BF16).rearrange("p (k c) -> p k c", c=256)

            def __getitem__(self, idx):
                return self.ap3[idx]

        wslots = [(wring[0], wring[0]), (wring[1], wring[1]), (res, WView(res)), (ftmp, WView(ftmp)), (xbuf[0], WView(xbuf[0]))]

        def stream_piece(pi):
            tok_, wb = wslots[wri[0] % 5]
            wri[0] += 1
            dma('sp', wb[:].rearrange("p k c -> p (k c)"), wsc[pi, :, :], writes=[tok_])
            return tok_, wb

        def ssm_A(pc, t0, i):
            nt = 64
            if ssm_mode['pc']:
                bur = PC[0]; bui = PC[0]; bo = 256
            else:
                bur = PA[(2 * i) % 3]; bui = PA[(2 * i + 1) % 3]; bo = 0
            gr = gr2[i % 2]; gi = gi2[i % 2]
            for q in range(4):
                op('pe', lambda t: t.matmul(bur[:, q * 64:q * 64 + nt], lhsT=wbz[:, pc, q, 0, :], rhs=ub[:, pc, t0:t0 + nt], start=True, stop=True),
                   reads=[wbz, ub], writes=[bur], inc=False)
                op('pe', lambda t: t.matmul(bui[:, bo + q * 64:bo + q * 64 + nt], lhsT=wbz[:, pc, q, 1, :], rhs=ub[:, pc, t0:t0 + nt], start=True, stop=True),
                   reads=[wbz, ub], writes=[bui], inc=(q == 3))
            bur3 = bur[:, 0:256].rearrange("p (q t) -> p q t", t=64)
            bui3 = bui[:, bo:bo + 256].rearrange("p (q t) -> p q t", t=64)
            ct = cosT[:, 4 * pc:4 * pc + 4, 0:nt]
            sn = sinT[:, 4 * pc:4 * pc + 4, 0:nt]
            op('dve', lambda v: v.tensor_tensor(out=mm[0][:], in0=bur3, in1=ct, op=ALU.mult), reads=[bur, cosT], writes=[mm[0]])
            op('dve', lambda v: v.tensor_tensor(out=mm[1][:], in0=bui3, in1=sn, op=ALU.mult), reads=[bui, sinT], writes=[mm[1]])
            op('dve', lambda v: v.tensor_tensor(out=xr[:], in0=mm[0][:], in1=mm[1][:], op=ALU.add), reads=[mm[0], mm[1]], writes=[xr])
            op('dve', lambda v: v.tensor_tensor(out=mm[0][:], in0=bui3, in1=ct, op=ALU.mult), reads=[bui, cosT, mm[0]], writes=[mm[0]])
            op('dve', lambda v: v.tensor_tensor(out=mm[1][:], in0=bur3, in1=sn, op=ALU.mult), reads=[bur, sinT, mm[1]], writes=[mm[1]])
            op('dve', lambda v: v.tensor_tensor(out=xi[:], in0=mm[0][:], in1=mm[1][:], op=ALU.subtract), reads=[mm[0], mm[1]], writes=[xi])
            for q in range(4):
                j = 4 * pc + q
                rb = rho[:, j:j + 1].to_broadcast([128, nt])
                op('dve', lambda v: v.tensor_tensor_scan(out=gr[:, q, :], data0=rb, data1=xr[:, q, :], initial=hprev[:, 0, j:j + 1], op0=ALU.mult, op1=ALU.add),
                   reads=[xr, rho, hp_tok[pc]], writes=[gr])
                op('dve', lambda v: v.tensor_tensor_scan(out=gi[:, q, :], data0=rb, data1=xi[:, q, :], initial=hprev[:, 1, j:j + 1], op0=ALU.mult, op1=ALU.add),
                   reads=[xi, rho, hp_tok[pc]], writes=[gi])

        def ssm_B(pc, i):
            nt = 64
            gr = gr2[i % 2]; gi = gi2[i % 2]
            hr_ = hrb[i % 2]; hi_ = hib[i % 2]
            ct = cosT[:, 4 * pc:4 * pc + 4, 0:nt]
            sn = sinT[:, 4 * pc:4 * pc + 4, 0:nt]
            cl = cosT[:, 4 * pc:4 * pc + 4, nt - 1]
            sl = sinT[:, 4 * pc:4 * pc + 4, nt - 1]
            op('pool', lambda v: v.tensor_tensor(out=cty[0][:], in0=gr[:, :, nt - 1], in1=cl, op=ALU.mult), reads=[gr, cosT], writes=[cty[0]])
            op('pool', lambda v: v.tensor_tensor(out=cty[1][:], in0=gi[:, :, nt - 1], in1=sl, op=ALU.mult), reads=[gi, sinT], writes=[cty[1]])
            op('pool', lambda v: v.tensor_tensor(out=cty[2][:], in0=gi[:, :, nt - 1], in1=cl, op=ALU.mult), reads=[gi, cosT], writes=[cty[2]])
            op('pool', lambda v: v.tensor_tensor(out=cty[3][:], in0=gr[:, :, nt - 1], in1=sl, op=ALU.mult), reads=[gr, sinT], writes=[cty[3]])
            op('pool', lambda v: v.tensor_tensor(out=hprev[:, 0, 4 * pc:4 * pc + 4], in0=cty[0][:], in1=cty[1][:], op=ALU.subtract),
               reads=[cty[0], cty[1]], writes=[hp_tok[pc]])
            op('pool', lambda v: v.tensor_tensor(out=hprev[:, 1, 4 * pc:4 * pc + 4], in0=cty[2][:], in1=cty[3][:], op=ALU.add),
               reads=[cty[2], cty[3]], writes=[hp_tok[pc]])
            d = [rt[k_][:].rearrange("p a b -> p (a b)").bitcast(BF16)[:, 0:256].rearrange("p (q t) -> p q t", t=64) for k_ in range(4)]
            op('pool', lambda v: v.tensor_tensor(out=d[0], in0=gr[:], in1=ct, op=ALU.mult), reads=[gr, cosT], writes=[rt[0]])
            op('pool', lambda v: v.tensor_tensor(out=d[1], in0=gi[:], in1=sn, op=ALU.mult), reads=[gi, sinT], writes=[rt[1]])
            op('pool', lambda v: v.tensor_tensor(out=hr_[:], in0=d[0], in1=d[1], op=ALU.subtract), reads=[rt[0], rt[1]], writes=[hr_])
            op('pool', lambda v: v.tensor_tensor(out=d[2], in0=gi[:], in1=ct, op=ALU.mult), reads=[gi, cosT], writes=[rt[2]])
            op('pool', lambda v: v.tensor_tensor(out=d[3], in0=gr[:], in1=sn, op=ALU.mult), reads=[gr, sinT], writes=[rt[3]])
            op('pool', lambda v: v.tensor_tensor(out=hi_[:], in0=d[2], in1=d[3], op=ALU.add), reads=[rt[2], rt[3]], writes=[hi_])
            return hr_, hi_

        ssm_cnt = [0]
        ssm_mode = {'pc': False}

        def front(c, pre=None):
            s_, is_prompt, b_, qt, xsrc, outs, slot = c
            tp = 128 if is_prompt else 64
            tok0 = qt * 128 if is_prompt else 0
            kt_new = qt if is_prompt else 8
            kc0 = kt_new * 128
            nk = kc0 + tp
            y_out, k_out, v_out, ki_out = outs
            xb = xbuf[0]
            qT = qT2[slot]; qiT = qiT2[slot]; Dm = Dm2[slot]; aw = aw2[slot]; sg8 = sg82[slot]
            sgaT = sgaT2[slot]; sgs = sgs2[slot]; sma = sma2[slot]; smb = smb2[slot]; ygb = ygb2[slot]
            if pre is not None:
                pre()
            dma('sp', xb[0:tp, :], xsrc[tok0:tok0 + tp, :], writes=[xb])
            op('act', lambda a: a.activation(out=ftmp[0:tp, :], in_=xb[0:tp, :], func=AF.Square, accum_out=st[0][0:tp, :]), reads=[xb], writes=[ftmp, st[0]])
            op('dve', lambda v: v.tensor_scalar(out=st[1][0:tp, :], in0=st[0][0:tp, :], scalar1=1.0 / D, scalar2=1e-6, op0=ALU.mult, op1=ALU.add),
               reads=[st[0]], writes=[st[1]])
            op('act', lambda a: a.activation(out=st[1][0:tp, :], in_=st[1][0:tp, :], func=AF.Sqrt), reads=[st[1]], writes=[st[1]])
            op('dve', lambda v: v.reciprocal(out=st[1][0:tp, :], in_=st[1][0:tp, :]), reads=[st[1]], writes=[st[1]])
            op('act', lambda a: a.activation(out=xb[0:tp, :], in_=xb[0:tp, :], func=AF.Identity, scale=st[1][0:tp, 0:1]), reads=[xb, st[1]], writes=[xb])
            for j in range(8):
                pa = nextA()
                op('pe', lambda t: t.transpose(out=pa[:, 0:tp], in_=xb[0:tp, j * 128:(j + 1) * 128], identity=idf[0:tp, 0:tp]), reads=[xb, idf], writes=[pa])
                op('act', lambda a: a.activation(out=hT[:, j, 0:tp], in_=pa[:, 0:tp], func=AF.Identity, scale=Amod[:, s_, j:j + 1], bias=Bmod[:, s_, j:j + 1]),
                   reads=[pa, Amod, Bmod], writes=[hT])
            yield 'x'
            blks = [(0, 512), (512, 1024), (1024, TOKW)]
            for (c0, c1) in blks:
                pa = nextA()
                for k in range(8):
                    op('pe', lambda t: t.matmul(pa[0:tp, 0:c1 - c0], lhsT=hT[:, k, 0:tp], rhs=win[:, k, c0:c1], start=(k == 0), stop=(k == 7)),
                       reads=[hT, win], writes=[pa], inc=(k == 7))
                op('act', lambda a: a.activation(out=tokm[0:tp, c0:c1], in_=pa[0:tp, 0:c1 - c0], func=AF.Copy), reads=[pa], writes=[tokm])
            yield 'tok'
            import os
            _dbg = int(os.environ.get("DBG_FM", "9"))
            for pi in range(NPIECE):
                wtok, wb = stream_piece(pi)
                if _dbg <= 0:
                    continue
                for f in range(2):
                    ft = 2 * pi + f
                    pa = nextA()
                    for k in range(8):
                        op('pe', lambda t: t.matmul(pa[:, 0:tp], lhsT=wb[:, k, f * 128:(f + 1) * 128], rhs=hT[:, k, 0:tp], start=(k == 0), stop=(k == 7)),
                           reads=[wtok, hT], writes=[pa], inc=(k == 7))
                    if _dbg <= 1:
                        continue
                    if _dbg <= 2:
                        op('act', lambda a: a.activation(out=sgaT[:, ft % 4, 0:tp], in_=pa[:, 0:tp], func=AF.Sigmoid), reads=[pa], writes=[sgaT])
                        continue
                    _grp = 0 if ft < 4 else 1 if ft < 8 else 2 if ft < 12 else 3 if ft < 20 else 4
                    if not (int(os.environ.get("DBG_FT", "31")) >> _grp) & 1:
                        continue
                    if ft < 4:
                        op('act', lambda a: a.activation(out=sgl[:, 0:tp], in_=pa[:, 0:tp], func=AF.Sigmoid), reads=[pa], writes=[sgl])
                        op('act', lambda a: a.activation(out=gt1[:, 0:tp], in_=pa[:, 0:tp], func=AF.Copy), reads=[pa], writes=[gt1])
                        op('pool', lambda v: v.tensor_tensor(out=sgaT[:, ft, 0:tp], in0=gt1[:, 0:tp], in1=sgl[:, 0:tp], op=ALU.mult), reads=[gt1, sgl], writes=[sgaT])
                    elif ft < 8:
                        op('act', lambda a: a.activation(out=uf[:, ft - 4, 0:tp], in_=pa[:, 0:tp], func=AF.Copy), reads=[pa], writes=[uf])
                        op('pool', lambda v: v.tensor_copy(out=ub[:, ft - 4, 0:tp], in_=uf[:, ft - 4, 0:tp]), reads=[uf], writes=[ub])
                    elif ft < 12:
                        op('act', lambda a: a.activation(out=sgl[:, 0:tp], in_=pa[:, 0:tp], func=AF.Sigmoid), reads=[pa], writes=[sgl])
                        op('act', lambda a: a.activation(out=gt1[:, 0:tp], in_=pa[:, 0:tp], func=AF.Copy), reads=[pa], writes=[gt1])
                        op('pool', lambda v: v.tensor_tensor(out=sgs[:, ft - 8, 0:tp], in0=gt1[:, 0:tp], in1=sgl[:, 0:tp], op=ALU.mult), reads=[gt1, sgl], writes=[sgs])
                    elif ft < 20:
                        op('act', lambda a: a.activation(out=sma[:, ft - 12, 0:tp], in_=pa[:, 0:tp], func=AF.Sigmoid), reads=[pa], writes=[sma])
                    else:
                        op('act', lambda a: a.activation(out=smb[:, ft - 20, 0:tp], in_=pa[:, 0:tp], func=AF.Sigmoid), reads=[pa], writes=[smb])
            yield 'x'
            ci = qt if is_prompt else 16
            for (b0, nb) in ((0, 10), (12, 9)):
                blk = tokm[0:tp, 0:1344].rearrange("p (b d) -> p b d", d=64)
                x1 = blk[:, b0:b0 + nb, 0:8]
                x2 = blk[:, b0:b0 + nb, 8:16]
                cb = ropeC[0:tp, ci:ci + 1, :].to_broadcast([tp, nb, 8])
                sbb = ropeS[0:tp, ci:ci + 1, :].to_broadcast([tp, nb, 8])
                r_ = [x_[0:tp, 0:nb, :] for x_ in rt]
                op('pool', lambda v: v.tensor_tensor(out=r_[0], in0=x1, in1=cb, op=ALU.mult), reads=[tokm, ropeC], writes=[rt[0]])
                op('pool', lambda v: v.tensor_tensor(out=r_[1], in0=x2, in1=sbb, op=ALU.mult), reads=[tokm, ropeS], writes=[rt[1]])
                op('pool', lambda v: v.tensor_tensor(out=r_[2], in0=x2, in1=cb, op=ALU.mult), reads=[tokm, ropeC], writes=[rt[2]])
                op('pool', lambda v: v.tensor_tensor(out=r_[3], in0=x1, in1=sbb, op=ALU.mult), reads=[tokm, ropeS], writes=[rt[3]])
                op('pool', lambda v: v.tensor_tensor(out=x1, in0=r_[0], in1=r_[1], op=ALU.subtract), reads=[rt[0], rt[1], tokm], writes=[tokm])
                op('pool', lambda v: v.tensor_tensor(out=x2, in0=r_[2], in1=r_[3], op=ALU.add), reads=[rt[2], rt[3], tokm], writes=[tokm])
            yield 'x'
            dma('act', k_out[tok0:tok0 + tp, :], tokm[0:tp, 512:640], reads=[tokm])
            dma('act', v_out[tok0:tok0 + tp, :], tokm[0:tp, 640:768], reads=[tokm])
            dma('act', ki_out[tok0:tok0 + tp, :], tokm[0:tp, 1280:1344], reads=[tokm])
            yield 'x'
            op('act', lambda a: a.activation(out=qb[0:tp, :].rearrange("p (r g d) -> p g r d", r=4, g=2),
                                             in_=tokm[0:tp, 0:512].rearrange("p (g r d) -> p g r d", g=2, r=4), func=AF.Copy, scale=0.125),
               reads=[tokm], writes=[qb])
            op('act', lambda a: a.activation(out=qib[0:tp, :], in_=tokm[0:tp, 768:1280], func=AF.Copy), reads=[tokm], writes=[qib])
            op('pool', lambda v: v.tensor_copy(out=kb2[0:tp, :], in_=tokm[0:tp, 512:640]), reads=[tokm], writes=[kb2])
            op('pool', lambda v: v.tensor_copy(out=kib[0:tp, 0, :], in_=tokm[0:tp, 1280:1344]), reads=[tokm], writes=[kib])
            op('pool', lambda v: v.tensor_copy(out=kib[0:tp, 1, :], in_=tokm[0:tp, 1280:1344]), reads=[tokm, kib], writes=[kib])
            op('pool', lambda v: v.tensor_copy(out=vaug[kt_new][0:tp, :, 0:64], in_=tokm[0:tp, 640:768].rearrange("p (g d) -> p g d", d=64)),
               reads=[tokm], writes=[vaug[kt_new]])
            pb_ = nextB()
            pbb = pb_[:].bitcast(BF16)
            op('pe', lambda t: t.transpose(out=pbb[:, 0:tp], in_=kb2[0:tp, :], identity=idb[0:tp, 0:tp]), reads=[kb2, idb], writes=[pb_])
            op('act', lambda a: a.activation(out=kT[:, kc0:kc0 + tp], in_=pbb[:, 0:tp], func=AF.Copy), reads=[pb_], writes=[kT_tok[kt_new]])
            pb_ = nextB()
            pbb2 = pb_[:].bitcast(BF16)
            op('pe', lambda t: t.transpose(out=pbb2[:, 0:tp], in_=kib[0:tp, :, :].rearrange("p a d -> p (a d)"), identity=idb[0:tp, 0:tp]), reads=[kib, idb], writes=[pb_])
            op('act', lambda a: a.activation(out=kiT[:, kc0:kc0 + tp], in_=pbb2[:, 0:tp], func=AF.Copy), reads=[pb_], writes=[kiT_tok[kt_new]])
            for r in range(4):
                pb_ = nextB()
                pq = pb_[:].bitcast(BF16)
                op('pe', lambda t: t.transpose(out=pq[:, 0:tp], in_=qb[0:tp, r * 128:(r + 1) * 128], identity=idb[0:tp, 0:tp]), reads=[qb, idb], writes=[pb_])
                op('act', lambda a: a.activation(out=qT[:, r, 0:tp], in_=pq[:, 0:tp], func=AF.Copy), reads=[pb_], writes=[qT])
            for j in range(4):
                pb_ = nextB()
                pq2 = pb_[:].bitcast(BF16)
                op('pe', lambda t: t.transpose(out=pq2[:, 0:tp], in_=qib[0:tp, j * 128:(j + 1) * 128], identity=idb[0:tp, 0:tp]), reads=[qib, idb], writes=[pb_])
                op('act', lambda a: a.activation(out=qiT[:, j, 0:tp], in_=pq2[:, 0:tp], func=AF.Copy), reads=[pb_], writes=[qiT])
            yield 'x'
            op('act', lambda a: a.activation(out=aw[0:tp, :], in_=tokm[0:tp, 1344:1352], func=AF.Abs, scale=float(8 ** -0.5) * 0.125),
               reads=[tokm], writes=[aw])
            op('pool', lambda v: v.tensor_scalar(out=sg8[0:tp, :], in0=tokm[0:tp, 1344:1352], scalar1=0.0, scalar2=2.0, op0=ALU.is_ge, op1=ALU.mult),
               reads=[tokm], writes=[sg8])
            op('pool', lambda v: v.tensor_scalar(out=sg8[0:tp, :], in0=sg8[0:tp, :], scalar1=-1.0, scalar2=None, op0=ALU.add), reads=[sg8], writes=[sg8])
            for h in range(8):
                op('pool', lambda v: v.tensor_scalar(out=Dm[0:tp, h, 0:tp], in0=idb[0:tp, 0:tp], scalar1=sg8[0:tp, h:h + 1], scalar2=None, op0=ALU.mult),
                   reads=[idb, sg8], writes=[Dm])
            yield 'pre_ssm'
            nh = tp // 64
            units = [(hf, pc) for hf in range(nh) for pc in range(4)]
            i0 = ssm_cnt[0]
            def ssm_C(pc, t0, hr_, hi_):
                yps = PC[1] if ssm_mode['pc'] else PA[3]
                for q in range(4):
                    op('pe', lambda t: t.matmul(yps[:, 0:64], lhsT=cz[:, pc, q, 0, :], rhs=hr_[:, q, :], start=(q == 0), stop=False),
                       reads=[cz, hr_], writes=[yps], inc=False)
                    op('pe', lambda t: t.matmul(yps[:, 0:64], lhsT=cz[:, pc, q, 1, :], rhs=hi_[:, q, :], start=False, stop=(q == 3)),
                       reads=[cz, hi_], writes=[yps], inc=(q == 3))
                ts_ = slice(t0, t0 + 64)
                op('dve', lambda v: v.scalar_tensor_tensor(out=ysum[:, ts_], in0=uf[:, pc, ts_], scalar=dsk[:, pc:pc + 1], in1=yps[:, 0:64], op0=ALU.mult, op1=ALU.add),
                   reads=[uf, dsk, yps], writes=[ysum])
                op('dve', lambda v: v.tensor_tensor(out=yt1[:, ts_], in0=ysum[:, ts_], in1=ysum[:, ts_], op=ALU.mult), reads=[ysum], writes=[yt1])
                op('act', lambda a: a.activation(out=yt1[:, ts_], in_=yt1[:, ts_], func=AF.Identity, scale=0.044715, bias=one_c[:, 0:1]), reads=[yt1, one_c], writes=[yt1])
                op('dve', lambda v: v.tensor_tensor(out=yt1[:, ts_], in0=yt1[:, ts_], in1=ysum[:, ts_], op=ALU.mult), reads=[yt1, ysum], writes=[yt1])
                op('act', lambda a: a.activation(out=yt1[:, ts_], in_=yt1[:, ts_], func=AF.Relu, bias=one_c[:, 1:2]), reads=[yt1, one_c], writes=[yt1])
                op('act', lambda a: a.activation(out=yt2[:, ts_], in_=yt1[:, ts_], func=AF.Exp, scale=-1.5957691216057308, bias=one_c[:, 2:3]), reads=[yt1, one_c], writes=[yt2])
                op('dve', lambda v: v.tensor_scalar(out=yt2[:, ts_], in0=yt2[:, ts_], scalar1=1.0, scalar2=None, op0=ALU.add), reads=[yt2], writes=[yt2])
                op('dve', lambda v: v.reciprocal(out=yt2[:, ts_], in_=yt2[:, ts_]), reads=[yt2], writes=[yt2])
                op('dve', lambda v: v.tensor_tensor(out=ygb[:, pc, ts_], in0=ysum[:, ts_], in1=yt2[:, ts_], op=ALU.mult), reads=[ysum, yt2], writes=[ygb])

            ssm_A(units[0][1], units[0][0] * 64, i0)
            prevc = None
            for ui, (hf, pc) in enumerate(units):
                i = i0 + ui
                t0 = hf * 64
                if prevc is not None:
                    ssm_C(*prevc)
                if ui + 1 < len(units):
                    ssm_A(units[ui + 1][1], units[ui + 1][0] * 64, i + 1)
                hr_, hi_ = ssm_B(pc, i)
                prevc = (pc, t0, hr_, hi_)
                yield 'ssm'
            ssm_C(*prevc)
            ssm_cnt[0] += len(units)
            yield 'x'

        def idxgen(c):
            s_, is_prompt, b_, qt, xsrc, outs, slot = c
            tp = 128 if is_prompt else 64
            tok0 = qt * 128 if is_prompt else 0
            kt_new = qt if is_prompt else 8
            kc0 = kt_new * 128
            nk = kc0 + tp
            y_out, k_out, v_out, ki_out = outs
            xb = xbuf[0]
            qT = qT2[slot]; qiT = qiT2[slot]; Dm = Dm2[slot]; aw = aw2[slot]; sg8 = sg82[slot]
            sgaT = sgaT2[slot]; sgs = sgs2[slot]; sma = sma2[slot]; smb = smb2[slot]; ygb = ygb2[slot]
            nkb = (nk + 511) // 512
            items = [(kb_, h) for kb_ in range(nkb) for h in range(8)]

            def emit_s(i):
                kb_, h = items[i]
                c0 = kb_ * 512
                w_ = min(512, nk - c0)
                ktoks = kiT_tok[c0 // 128:(c0 + w_ + 127) // 128]
                pr = 64 * (h % 2)
                sp_ = PB[i % 2]
                rl = relu[i % 2]
                op('pe', lambda t: t.matmul(sp_[0:tp, 0:w_], lhsT=qiT[pr:pr + 64, h // 2, 0:tp], rhs=kiT[pr:pr + 64, c0:c0 + w_], start=True, stop=True),
                   reads=[qiT] + ktoks, writes=[sp_])
                if i % 2 == 0:
                    op('act', lambda a: a.activation(out=rl[0:tp, 0:w_], in_=sp_[0:tp, 0:w_], func=AF.Relu, scale=aw[0:tp, h:h + 1]), reads=[sp_, aw], writes=[rl])
                else:
                    op('dve', lambda v: v.tensor_scalar(out=rl[0:tp, 0:w_], in0=sp_[0:tp, 0:w_], scalar1=aw[0:tp, h:h + 1], scalar2=0.0, op0=ALU.mult, op1=ALU.max),
                       reads=[sp_, aw], writes=[rl])

            emit_s(0)
            for i in range(len(items)):
                if i + 1 < len(items):
                    emit_s(i + 1)
                kb_, h = items[i]
                c0 = kb_ * 512
                w_ = min(512, nk - c0)
                sc_ps = PA[kb_ % 4]
                rl = relu[i % 2]
                op('pe', lambda t: t.matmul(sc_ps[0:tp, 0:w_], lhsT=Dm[0:tp, h, 0:tp], rhs=rl[0:tp, 0:w_], start=(h == 0), stop=(h == 7)),
                   reads=[Dm, rl], writes=[sc_ps])
                if h == 7:
                    op('act', lambda a: a.activation(out=score[0:tp, c0:c0 + w_], in_=sc_ps[0:tp, 0:w_], func=AF.Copy), reads=[sc_ps], writes=[score])
                    yield 'idxblk'
            if is_prompt:
                op('pool', lambda g: g.memset(score[0:64, nk - 64:nk], -1e30), reads=[score], writes=[score])

        def back(c):
            s_, is_prompt, b_, qt, xsrc, outs, slot = c
            tp = 128 if is_prompt else 64
            tok0 = qt * 128 if is_prompt else 0
            kt_new = qt if is_prompt else 8
            kc0 = kt_new * 128
            nk = kc0 + tp
            y_out, k_out, v_out, ki_out = outs
            xb = xbuf[0]
            qT = qT2[slot]; qiT = qiT2[slot]; Dm = Dm2[slot]; aw = aw2[slot]; sg8 = sg82[slot]
            sgaT = sgaT2[slot]; sgs = sgs2[slot]; sma = sma2[slot]; smb = smb2[slot]; ygb = ygb2[slot]
            need_topk = (not is_prompt) or (qt >= 2)
            if need_topk:
                nlo = nk - 64 if is_prompt else nk
                op('dve', lambda v: v.tensor_reduce(out=st[2][0:tp, :], in_=score[0:tp, 0:nk], axis=AX.X, op=ALU.max), reads=[score], writes=[st[2]])
                op('dve', lambda v: v.tensor_reduce(out=blo[0:tp, :], in_=score[0:tp, 0:nlo], axis=AX.X, op=ALU.min), reads=[score], writes=[blo])
                op('dve', lambda v: v.tensor_tensor(out=st[3][0:tp, :], in0=st[2][0:tp, :], in1=blo[0:tp, :], op=ALU.subtract), reads=[st[2], blo], writes=[st[3]])
                op('dve', lambda v: v.tensor_scalar(out=bhalf[0:tp, :], in0=btab[0:tp, :], scalar1=st[3][0:tp, 0:1], scalar2=None, op0=ALU.mult),
                   reads=[st[3], btab], writes=[bhalf])
                op('dve', lambda v: v.tensor_tensor(out=bmid[0:tp, :], in0=blo[0:tp, :], in1=bhalf[0:tp, 0:1], op=ALU.add), reads=[blo, bhalf], writes=[bmid])
                for it in range(NITER):
                    op('dve', lambda v: v.tensor_scalar(out=negm[0:tp, 0:nk], in0=score[0:tp, 0:nk], scalar1=bmid[0:tp, 0:1], scalar2=0.0,
                                                         op0=ALU.is_gt, op1=ALU.add, accum_out=bcnt[0:tp, :]), reads=[score, bmid], writes=[negm, bcnt])
                    op('dve', lambda v: v.scalar_tensor_tensor(out=btmp[0:tp, :], in0=bcnt[0:tp, :], scalar=256.0, in1=bhalf[0:tp, it:it + 1],
                                                                op0=ALU.is_ge, op1=ALU.mult), reads=[bcnt, bhalf], writes=[btmp])
                    op('dve', lambda v: v.scalar_tensor_tensor(out=bmid[0:tp, :], in0=bmid[0:tp, :], scalar=bhalf[0:tp, it + 1:it + 2], in1=btmp[0:tp, :],
                                                                op0=ALU.subtract, op1=ALU.add), reads=[bmid, bhalf, btmp], writes=[bmid])
                op('dve', lambda v: v.tensor_tensor(out=blo[0:tp, :], in0=bmid[0:tp, :], in1=bhalf[0:tp, NITER:NITER + 1], op=ALU.subtract),
                   reads=[bmid, bhalf], writes=[blo])
                op('dve', lambda v: v.tensor_scalar(out=negm[0:tp, 0:nk], in0=score[0:tp, 0:nk], scalar1=blo[0:tp, 0:1], scalar2=NEG, op0=ALU.is_le, op1=ALU.mult),
                   reads=[score, blo], writes=[negm])
            else:
                op('dve', lambda v: v.tensor_scalar(out=negm[0:tp, 0:nk], in0=score[0:tp, 0:nk], scalar1=-1e29, scalar2=NEG, op0=ALU.is_le, op1=ALU.mult),
                   reads=[score], writes=[negm])
            yield 'bis'
            for g in range(2):
                op('pe', lambda t: t.matmul(PC[g][0:tp, 0:260], lhsT=zb[:, 0:tp], rhs=zb[:, 0:260], start=True, stop=True), reads=[zb], writes=[PC[g]])
            nkt = (nk + 127) // 128
            aitems = [(kt, g) for kt in range(nkt) for g in range(2)]

            def emit_qk(i):
                kt, g = aitems[i]
                k0 = kt * 128
                kw = min(128, nk - k0)
                lp = PB[i % 2]
                pt_ = pT[i % 3]
                op('pe', lambda t: t.matmul(lp[0:kw, 0:4 * tp], lhsT=kT[64 * g:64 * g + 64, k0:k0 + kw], rhs=qT[64 * g:64 * g + 64, :, 0:tp],
                                             start=True, stop=False), reads=[kT_tok[kt], qT], writes=[lp], inc=False)
                op('pe', lambda t: t.matmul(lp[0:kw, 0:4 * tp], lhsT=negm[:, k0:k0 + kw], rhs=id4[:, :, 0:tp], start=False, stop=True),
                   reads=[negm, id4], writes=[lp])
                op('act', lambda a: a.activation(out=pt_[0:kw, :, 0:tp], in_=lp[0:kw, 0:4 * tp].rearrange("p (r t) -> p r t", r=4), func=AF.Exp),
                   reads=[lp], writes=[pt_])

            emit_qk(0)
            for i in range(len(aitems)):
                if i + 1 < len(aitems):
                    emit_qk(i + 1)
                kt, g = aitems[i]
                k0 = kt * 128
                kw = min(128, nk - k0)
                pt_ = pT[i % 3]
                for r in range(4):
                    op('pe', lambda t: t.matmul(PC[g][0:tp, r * 65:(r + 1) * 65], lhsT=pt_[0:kw, r, 0:tp], rhs=vaug[kt][0:kw, g, :],
                                                 start=False, stop=True, skip_group_check=True), reads=[pt_, vaug[kt]], writes=[PC[g]], inc=(r == 3))
                if g == 1:
                    yield 'att'
            for g in range(2):
                op('act', lambda a: a.activation(out=osb[0:tp, 4 * g:4 * g + 4, :], in_=PC[g][0:tp, 0:260].rearrange("p (r d) -> p r d", d=65), func=AF.Copy),
                   reads=[PC[g]], writes=[osb])
            op('dve', lambda v: v.reciprocal(out=rden[0:tp, :], in_=osb[0:tp, :, 64]), reads=[osb], writes=[rden])
            op('dve', lambda v: v.tensor_tensor(out=og[0:tp, :].rearrange("p (h d) -> p h d", d=64), in0=osb[0:tp, :, 0:64],
                                                 in1=rden[0:tp, :].unsqueeze(2).to_broadcast([tp, 8, 64]), op=ALU.mult), reads=[osb, rden], writes=[og])
            yield 'att_done'
            for j in range(4):
                pb_ = nextB()
                pq3 = pb_[:].bitcast(BF16)
                op('pe', lambda t: t.transpose(out=pq3[:, 0:tp], in_=og[0:tp, j * 128:(j + 1) * 128], identity=idb[0:tp, 0:tp]), reads=[og, idb], writes=[pb_])
                op('dve', lambda v: v.tensor_tensor(out=ogT[:, j, 0:tp], in0=pq3[:, 0:tp], in1=sgaT[:, j, 0:tp], op=ALU.mult), reads=[pb_, sgaT], writes=[ogT])
            yield 'x'
            for of in range(4):
                pa1 = nextA(); pa2 = nextA()
                for k in range(4):
                    op('pe', lambda t: t.matmul(pa1[:, 0:tp], lhsT=wglu[:, k, of * 128:(of + 1) * 128], rhs=ygb[:, k, 0:tp], start=(k == 0), stop=(k == 3)),
                       reads=[wglu, ygb], writes=[pa1], inc=(k == 3))
                for k in range(4):
                    op('pe', lambda t: t.matmul(pa2[:, 0:tp], lhsT=wglu[:, k, (of + 4) * 128:(of + 5) * 128], rhs=ygb[:, k, 0:tp], start=(k == 0), stop=(k == 3)),
                       reads=[wglu, ygb], writes=[pa2], inc=(k == 3))
                op('act', lambda a: a.activation(out=sgl[:, 0:tp], in_=pa2[:, 0:tp], func=AF.Sigmoid), reads=[pa2], writes=[sgl])
                op('dve', lambda v: v.tensor_tensor(out=gt1[:, 0:tp], in0=pa1[:, 0:tp], in1=sgl[:, 0:tp], op=ALU.mult), reads=[pa1, sgl], writes=[gt1])
                op('dve', lambda v: v.tensor_tensor(out=yglu[:, of, 0:tp], in0=gt1[:, 0:tp], in1=sgs[:, of, 0:tp], op=ALU.mult), reads=[gt1, sgs], writes=[yglu])
            yield 'x'
            for of in range(8):
                pa1 = nextA(); pa2 = nextA()
                for k in range(4):
                    op('pe', lambda t: t.matmul(pa1[:, 0:tp], lhsT=wap[:, k, of * 128:(of + 1) * 128], rhs=ogT[:, k, 0:tp], start=(k == 0), stop=(k == 3)),
                       reads=[wap, ogT], writes=[pa1], inc=(k == 3))
                for k in range(4):
                    op('pe', lambda t: t.matmul(pa2[:, 0:tp], lhsT=wsp[:, k, of * 128:(of + 1) * 128], rhs=yglu[:, k, 0:tp], start=(k == 0), stop=(k == 3)),
                       reads=[wsp, yglu], writes=[pa2], inc=(k == 3))
                op('dve', lambda v: v.tensor_tensor(out=bat[:, 0:tp], in0=pa1[:, 0:tp], in1=sma[:, of, 0:tp], op=ALU.mult), reads=[pa1, sma], writes=[bat])
                op('dve', lambda v: v.tensor_tensor(out=mgt[:, 0:tp], in0=pa2[:, 0:tp], in1=smb[:, of, 0:tp], op=ALU.mult), reads=[pa2, smb], writes=[mgt])
                op('dve', lambda v: v.tensor_tensor(out=merged[:, of, 0:tp], in0=bat[:, 0:tp], in1=mgt[:, 0:tp], op=ALU.add), reads=[bat, mgt], writes=[merged])
            yield 'x'
            dma('sp', res[0:tp, :], xsrc[tok0:tok0 + tp, :], writes=[res])
            for blk in range(2):
                pa = nextA()
                for k in range(8):
                    op('pe', lambda t: t.matmul(pa[0:tp, :], lhsT=merged[:, k, 0:tp], rhs=wout[:, k, blk * 512:(blk + 1) * 512], start=(k == 0), stop=(k == 7)),
                       reads=[merged, wout], writes=[pa], inc=(k == 7))
                op('dve', lambda v: v.tensor_tensor(out=ftmp[0:tp, blk * 512:(blk + 1) * 512], in0=pa[0:tp, :], in1=gateB[0:tp, blk * 512:(blk + 1) * 512], op=ALU.mult),
                   reads=[pa, gateB], writes=[ftmp])
            op('dve', lambda v: v.tensor_tensor(out=res[0:tp, :], in0=res[0:tp, :], in1=ftmp[0:tp, :], op=ALU.add), reads=[res, ftmp], writes=[res])
            op('act', lambda a: a.activation(out=ftmp[0:tp, :], in_=res[0:tp, :], func=AF.Square, accum_out=st[0][0:tp, :]), reads=[res], writes=[ftmp, st[0]])
            op('dve', lambda v: v.tensor_scalar(out=st[1][0:tp, :], in0=st[0][0:tp, :], scalar1=1.0 / D, scalar2=1e-6, op0=ALU.mult, op1=ALU.add),
               reads=[st[0]], writes=[st[1]])
            op('act', lambda a: a.activation(out=st[1][0:tp, :], in_=st[1][0:tp, :], func=AF.Sqrt), reads=[st[1]], writes=[st[1]])
            op('dve', lambda v: v.reciprocal(out=st[1][0:tp, :], in_=st[1][0:tp, :]), reads=[st[1]], writes=[st[1]])
            op('dve', lambda v: v.scalar_tensor_tensor(out=ftmp[0:tp, :], in0=res[0:tp, :], scalar=st[1][0:tp, 0:1], in1=gfB[0:tp, :], op0=ALU.mult, op1=ALU.mult),
               reads=[res, st[1], gfB], writes=[ftmp])
            dma('act', y_out[tok0:tok0 + tp, :], ftmp[0:tp, :], reads=[ftmp])

        def begin_seq(s_, h0src):
            dma('sp', gateB[:, :], modscr[s_:s_ + 1, 2 * D:3 * D].to_broadcast([128, D]), reads=[scr_tok], writes=[gateB])
            if h0src is None:
                op('pool', lambda g: g.memset(hprev[:], 0.0), reads=hp_tok, writes=hp_tok)
            else:
                dma('sp', hprev[:, 0, :], h0src[0].rearrange("(j p) -> p j", p=128), writes=hp_tok)
                dma('sp', hprev[:, 1, :], h0src[1].rearrange("(j p) -> p j", p=128), writes=hp_tok)

        def end_seq(hr_dst, hi_dst):
            dma('act', hr_dst.rearrange("(j p) -> p j", p=128), hprev[:, 0, :], reads=hp_tok)
            dma('act', hi_dst.rearrange("(j p) -> p j", p=128), hprev[:, 1, :], reads=hp_tok)

        cs_tok = [Tok("cs%d" % i) for i in range(6)]

        def sample_prep(b_):
            op('pool', lambda g: g.memset(ftmp[0:1, 0:1], 0.0), writes=[ftmp])
            for kt in range(8):
                base = (kt % 2) * 3
                sk, ski, sv = cs_tok[base], cs_tok[base + 1], cs_tok[base + 2]
                ak = ftmp[:, base * 128:(base + 1) * 128]
                aki = ftmp[:, (base + 1) * 128:(base + 1) * 128 + 64]
                av = ftmp[:, (base + 2) * 128:(base + 3) * 128]
                dma('sp', ak, ck[b_, kt * 128:(kt + 1) * 128, :], reads=[ftmp], writes=[sk])
                dma('act', aki, cki[b_, kt * 128:(kt + 1) * 128, :], reads=[ftmp], writes=[ski])
                dma('sp', av, cv[b_, kt * 128:(kt + 1) * 128, :], reads=[ftmp], writes=[sv])
                op('dve', lambda v: v.tensor_copy(out=kb2[:, :], in_=ak), reads=[sk, ftmp], writes=[kb2])
                pb_ = nextB()
                pbb = pb_[:].bitcast(BF16)
                op('pe', lambda t: t.transpose(out=pbb[:, 0:128], in_=kb2[:, :], identity=idb[:, :]), reads=[kb2, idb], writes=[pb_])
                op('act', lambda a: a.activation(out=kT[:, kt * 128:(kt + 1) * 128], in_=pbb[:, 0:128], func=AF.Copy), reads=[pb_], writes=[kT_tok[kt]])
                op('dve', lambda v: v.tensor_copy(out=kib[:, 0, :], in_=aki), reads=[ski, ftmp], writes=[kib])
                op('dve', lambda v: v.tensor_copy(out=kib[:, 1, :], in_=aki), reads=[ski, ftmp, kib], writes=[kib])
                pb_ = nextB()
                pbb2 = pb_[:].bitcast(BF16)
                op('pe', lambda t: t.transpose(out=pbb2[:, 0:128], in_=kib[:, :, :].rearrange("p a d -> p (a d)"), identity=idb[:, :]), reads=[kib, idb], writes=[pb_])
                op('act', lambda a: a.activation(out=kiT[:, kt * 128:(kt + 1) * 128], in_=pbb2[:, 0:128], func=AF.Copy), reads=[pb_], writes=[kiT_tok[kt]])
                op('pool', lambda v: v.tensor_copy(out=vaug[kt][:, :, 0:64], in_=av.rearrange("p (g d) -> p g d", d=64)), reads=[sv, ftmp], writes=[vaug[kt]])

        def drain(g):
            for _ in g:
                pass

        def run_until(g, tag):
            for t in g:
                if t == tag:
                    return True
            return False

        def interleave(f, b, cnext):
            ssm_mode['pc'] = False
            run_until(f, 'tok')
            run_until(b, 'bis')
            run_until(f, 'pre_ssm')
            fa = True
            nkt_ = (cnext[3] * 128 + 127) // 128 if cnext[1] else 9
            uy = min(8, int(nkt_ * 0.4 + 0.5))
            kdone = 0
            udone = 0
            while True:
                t = next(b, None)
                kdone += 1
                while fa and udone < uy and udone < (kdone * uy + nkt_ - 1) // max(nkt_, 1):
                    if next(f, None) is None:
                        fa = False
                    udone += 1
                if t is None or t == 'att_done':
                    break
            ssm_mode['pc'] = True
            g = idxgen(cnext)
            alive = [b, g] + ([f] if fa else [])
            while alive:
                for x_ in list(alive):
                    if next(x_, None) is None:
                        alive.remove(x_)
            ssm_mode['pc'] = False

        import os as _os
        ntile = [0]
        pend = None
        for b_ in range(0, 0 if _os.environ.get('DBG_SKIP_PROMPT') else 2):
            for qt in range(16):
                c = (b_, True, b_, qt, xp[b_], (yp[b_], kp[b_], vp[b_], kip[b_]), ntile[0] % 2)
                ntile[0] += 1
                pre = (lambda bb=b_: begin_seq(bb, None)) if qt == 0 else None
                if qt == 0 and pend is not None:
                    drain(pend)
                    pend = None
                f = front(c, pre)
                if pend is not None:
                    interleave(f, pend, c)
                else:
                    drain(f)
                    drain(idxgen(c))
                if qt == 15:
                    end_seq(hrp[b_, :], hip[b_, :])
                pend = back(c)
                if stage <= 5 + ntile[0] - 1:
                    drain(pend)
                    fw.barrier()
                    return nc
        if pend is not None:
            drain(pend)
            pend = None
        op('pool', lambda g: g.memset(negm[64:128, :], 0.0), reads=[negm], writes=[negm])
        for b_ in range(4):
            s_ = 2 + b_
            c = (s_, False, b_, 0, xs[b_], (ys[b_], ksm[b_], vsm[b_], kism[b_]), ntile[0] % 2)
            ntile[0] += 1

            def pre(bb=b_, ss=s_):
                begin_seq(ss, (h0r[bb, :], h0i[bb, :]))
                sample_prep(bb)
            if pend is not None:
                drain(pend)
                pend = None
            f = front(c, pre)
            drain(f)
            drain(idxgen(c))
            end_seq(hrs[b_, :], his[b_, :])
            pend = back(c)
        drain(pend)
        fw.barrier()
    return nc


_NC_CACHE = {}


def kernel(x_prompt, x_sample, cache_k, cache_v, cache_idx_k, state_ssm_re, state_ssm_im,
           c_prompt, c_sample, w_mod, b_mod, g_norm, w_in, lambda_re, lambda_im, log_dt,
           ssm_b_re, ssm_b_im, ssm_c_re, ssm_c_im, d_skip, w_glu, w_attn_proj, w_ssm_proj,
           w_out, g_final):
    f = lambda a: np.ascontiguousarray(np.asarray(a, dtype=np.float32))
    if 'nc' not in _NC_CACHE:
        _NC_CACHE['nc'] = build_nc()
    nc = _NC_CACHE['nc']
    x_prompt = f(x_prompt); x_sample = f(x_sample)
    cache_k = f(cache_k); cache_v = f(cache_v); cache_idx_k = f(cache_idx_k)
    state_ssm_re = f(state_ssm_re); state_ssm_im = f(state_ssm_im)
    c_prompt = f(c_prompt); c_sample = f(c_sample)
    shared = {
        "w_mod": f(w_mod)[0], "b_mod": f(b_mod), "g_norm": f(g_norm), "w_in": f(w_in)[0],
        "lam_re": f(lambda_re).reshape(1, 2048), "lam_im": f(lambda_im).reshape(1, 2048), "log_dt": f(log_dt),
        "b_re": f(ssm_b_re).reshape(2048, 16), "b_im": f(ssm_b_im).reshape(2048, 16),
        "c_re": f(ssm_c_re).reshape(512, 64), "c_im": f(ssm_c_im).reshape(512, 64),
        "d_skip": f(d_skip), "w_glu": f(w_glu)[0], "w_ap": f(w_attn_proj)[0], "w_sp": f(w_ssm_proj)[0],
        "w_out": f(w_out)[0], "g_final": f(g_final).reshape(1, D),
    }
    in_maps = []
    for c in range(NCORE):
        m = dict(shared)
        m["xp"] = x_prompt[2 * c:2 * c + 2]
        m["xs"] = x_sample[4 * c:4 * c + 4]
        m["ck"] = cache_k[0, 4 * c:4 * c + 4].reshape(4, PAST, 128)
        m["cv"] = cache_v[0, 4 * c:4 * c + 4].reshape(4, PAST, 128)
        m["cki"] = cache_idx_k[0, 4 * c:4 * c + 4]
        m["h0r"] = state_ssm_re[0, 4 * c:4 * c + 4].reshape(4, 2048)
        m["h0i"] = state_ssm_im[0, 4 * c:4 * c + 4].reshape(4, 2048)
        m["call"] = np.ascontiguousarray(np.concatenate([c_prompt[2 * c:2 * c + 2], c_sample[4 * c:4 * c + 4]], axis=0))
        in_maps.append({k: np.ascontiguousarray(v) for k, v in m.items()})
    if _NC_CACHE.get('dbg_one'):
        res = run_bass_kernel_spmd(nc, in_maps[:1], core_ids=[0])
        return res.results[0]
    res = run_bass_kernel_spmd(nc, in_maps, core_ids=list(range(NCORE)))
    R = res.results
    cat = lambda k: np.concatenate([np.asarray(r[k], dtype=np.float32) for r in R], axis=0)
    y_prompt = cat("yp")
    y_sample = cat("ys")
    k_prompt = cat("kp").reshape(1, 16, TP_, 2, 64)
    v_prompt = cat("vp").reshape(1, 16, TP_, 2, 64)
    ki_prompt = cat("kip").reshape(1, 16, TP_, 64)
    hr_p = cat("hrp").reshape(1, 16, 32, 64)
    hi_p = cat("hip").reshape(1, 16, 32, 64)
    k_s = cat("ksm").reshape(1, 32, TS_, 2, 64)
    v_s = cat("vsm").reshape(1, 32, TS_, 2, 64)
    ki_s = cat("kism").reshape(1, 32, TS_, 64)
    hr_s = cat("hrs").reshape(1, 32, 32, 64)
    hi_s = cat("his").reshape(1, 32, 32, 64)
    return (y_prompt, y_sample, k_prompt, v_prompt, ki_prompt, hr_p, hi_p, k_s, v_s, ki_s, hr_s, hi_s)
```

```python
import math
from contextlib import ExitStack

import numpy as np
import concourse.bass as bass
import concourse.mybir as mybir
from concourse.bass_utils import run_bass_kernel_spmd

F32 = mybir.dt.float32
BF16 = mybir.dt.bfloat16
I32 = mybir.dt.int32
ALU = mybir.AluOpType
AF = mybir.ActivationFunctionType
AX = mybir.AxisListType

D = 1024
NCORE = 8
TP_ = 2048
TS_ = 64
PAST = 1024
INW = 4936
TOKW = 1352
NPIECE = 14
NITER = 20
NEG = -30000.0
TWO_PI = 2.0 * math.pi


class Buf:
    def __init__(self, t, name):
        self.t = t
        self.name = name
        self.w = None
        self.r = {}

    def __getitem__(self, idx):
        return self.t[idx]


class Tok(Buf):
    def __init__(self, name):
        Buf.__init__(self, None, name)


class FW:
    def __init__(self, nc, es):
        self.nc = nc
        self.es = es
        self.eng = {'pe': nc.tensor, 'dve': nc.vector, 'act': nc.scalar, 'pool': nc.gpsimd, 'sp': nc.sync}
        self.sem = {k: es.enter_context(nc.semaphore('s_' + k)) for k in self.eng}
        self.cnt = {k: 0 for k in self.eng}
        self.seen = {k: {} for k in self.eng}
        self.ND = 24
        self.dsem = [es.enter_context(nc.semaphore('d%d' % i)) for i in range(self.ND)]
        self.dcnt = [0] * self.ND
        self.dnext = 0
        self.semobj = dict(self.sem)
        for i, s in enumerate(self.dsem):
            self.semobj['d%d' % i] = s
        self.ninstr = 0

    def sb(self, name, shape, dt, es=None):
        return Buf((es or self.es).enter_context(self.nc.sbuf_tensor(name, shape, dt)), name)

    def ps(self, name, shape, dt):
        return Buf(self.es.enter_context(self.nc.psum_tensor(name, shape, dt)), name)

    def _wait(self, e, key, val):
        if val is None or val <= 0:
            return
        if e == 'pe' and key == 'pe' and not getattr(self, 'pe_serial', False):
            return
        if self.seen[e].get(key, 0) >= val:
            return
        self.eng[e].wait_ge(self.semobj[key], val)
        self.seen[e][key] = val

    def deps(self, e, reads, writes):
        for b in reads:
            if b.w is not None:
                self._wait(e, *b.w)
        for b in writes:
            if b.w is not None:
                self._wait(e, *b.w)
            for k, v in b.r.items():
                self._wait(e, k, v)

    def mark(self, tick, reads, writes):
        for b in reads:
            b.r[tick[0]] = max(b.r.get(tick[0], 0), tick[1])
        for b in writes:
            b.w = tick
            b.r = {}

    def op(self, e, fn, reads=(), writes=(), inc=True):
        if getattr(self, 'pe_serial', False):
            inc = True
        self.deps(e, reads, writes)
        ins = fn(self.eng[e])
        self.ninstr += 1
        if inc:
            ins.then_inc(self.sem[e], 1)
            self.cnt[e] += 1
        tick = (e, self.cnt[e] if inc else self.cnt[e] + 1)
        self.mark(tick, reads, writes)
        return tick

    def dma(self, e, out, in_, reads=(), writes=(), **kw):
        i = self.dnext
        self.dnext = (self.dnext + 1) % self.ND
        key = 'd%d' % i
        self._wait(e, key, self.dcnt[i])
        self.deps(e, reads, writes)
        self.eng[e].dma_start(out=out, in_=in_, **kw).then_inc(self.dsem[i], 16)
        self.ninstr += 1
        self.dcnt[i] += 16
        tick = (key, self.dcnt[i])
        self.mark(tick, reads, writes)
        return tick

    def barrier(self):
        for e in self.eng:
            for k in self.eng:
                self._wait(e, k, self.cnt[k])
            for i in range(self.ND):
                self._wait(e, 'd%d' % i, self.dcnt[i])


class _Stop(Exception):
    pass


def build_nc(stage=99, sub=99):
    nc = bass.Bass("TRN2", target_bir_lowering=False)

    def din(name, shape, dt=F32):
        return nc.dram_tensor(name, list(shape), dt, kind="ExternalInput").ap()

    def dout(name, shape, dt=F32):
        return nc.dram_tensor(name, list(shape), dt, kind="ExternalOutput").ap()

    xp = din("xp", [2, TP_, D]); xs = din("xs", [4, TS_, D])
    ck = din("ck", [4, PAST, 128]); cv = din("cv", [4, PAST, 128]); cki = din("cki", [4, PAST, 64])
    h0r = din("h0r", [4, 2048]); h0i = din("h0i", [4, 2048])
    call = din("call", [6, D])
    w_mod = din("w_mod", [D, 3 * D]); b_mod = din("b_mod", [1, 3 * D]); g_norm = din("g_norm", [1, D])
    w_in = din("w_in", [D, INW])
    lam_re = din("lam_re", [1, 2048]); lam_im = din("lam_im", [1, 2048]); log_dt = din("log_dt", [1, 32])
    b_re = din("b_re", [2048, 16]); b_im = din("b_im", [2048, 16])
    c_re = din("c_re", [512, 64]); c_im = din("c_im", [512, 64])
    d_skip = din("d_skip", [1, 512])
    w_glu = din("w_glu", [512, D]); w_ap = din("w_ap", [512, D]); w_sp = din("w_sp", [512, D])
    w_out = din("w_out", [D, D]); g_final = din("g_final", [1, D])

    yp = dout("yp", [2, TP_, D]); ys = dout("ys", [4, TS_, D])
    kp = dout("kp", [2, TP_, 128]); vp = dout("vp", [2, TP_, 128]); kip = dout("kip", [2, TP_, 64])
    hrp = dout("hrp", [2, 2048]); hip = dout("hip", [2, 2048])
    ksm = dout("ksm", [4, TS_, 128]); vsm = dout("vsm", [4, TS_, 128]); kism = dout("kism", [4, TS_, 64])
    hrs = dout("hrs", [4, 2048]); his = dout("his", [4, 2048])

    modscr = nc.dram_tensor("modscr", [6, 3 * D], F32, kind="Internal").ap()
    wsc = nc.dram_tensor("wsc", [NPIECE, 128, 8 * 256], BF16, kind="Internal").ap()

    es = ExitStack()
    with es:
        nc_ctx = es.enter_context(nc.allow_non_contiguous_dma(reason="small strided setup loads"))
        fw = FW(nc, es)
        op = fw.op
        dma = fw.dma

        win = fw.sb("win", [128, 8, TOKW], BF16)
        wglu = fw.sb("wglu", [128, 4, D], BF16)
        wap = fw.sb("wap", [128, 4, D], BF16)
        wsp = fw.sb("wsp", [128, 4, D], BF16)
        wout = fw.sb("wout", [128, 8, D], BF16)
        wring = [fw.sb("wring%d" % i, [128, 8, 256], BF16) for i in range(2)]
        wbz = fw.sb("wbz", [128, 4, 4, 2, 128], BF16)
        cz = fw.sb("cz", [128, 4, 4, 2, 128], BF16)
        cosT = fw.sb("cosT", [128, 16, 64], F32)
        sinT = fw.sb("sinT", [128, 16, 64], F32)
        rho = fw.sb("rho", [128, 16], F32)
        dsk = fw.sb("dsk", [128, 4], F32)
        kT = fw.sb("kT", [128, 2176], BF16)
        kiT = fw.sb("kiT", [128, 2176], BF16)
        kT_tok = [Tok("kTt%d" % i) for i in range(17)]
        kiT_tok = [Tok("kiTt%d" % i) for i in range(17)]
        vaug = [fw.sb("vaug%d" % i, [128, 2, 65], BF16) for i in range(17)]
        idf = fw.sb("idf", [128, 128], F32)
        idb = fw.sb("idb", [128, 128], BF16)
        id4 = fw.sb("id4", [128, 4, 128], BF16)
        zb = fw.sb("zb", [128, 512], BF16)
        ropeC = fw.sb("ropeC", [128, 17, 8], F32)
        ropeS = fw.sb("ropeS", [128, 17, 8], F32)
        Amod = fw.sb("Amod", [128, 6, 8], F32)
        Bmod = fw.sb("Bmod", [128, 6, 8], F32)
        gfB = fw.sb("gfB", [128, D], F32)
        gateB = fw.sb("gateB", [128, D], F32)
        hprev = fw.sb("hprev", [128, 2, 16], F32)
        hp_tok = [Tok("hp%d" % i) for i in range(4)]

        PA = [fw.ps("psA%d" % i, [128, 512], F32) for i in range(4)]
        PB = [fw.ps("psB%d" % i, [128, 512], F32) for i in range(2)]
        PC = [fw.ps("psC%d" % i, [128, 512], F32) for i in range(2)]
        ringA = [0]

        def nextA():
            b = PA[ringA[0] % 4]
            ringA[0] += 1
            return b
        ringB = [0]

        def nextB():
            b = PB[ringB[0] % 2]
            ringB[0] += 1
            return b

        op('pool', lambda g: g.memset(idf[:], 0.0), writes=[idf])
        op('pool', lambda g: g.affine_select(out=idf[:], in_=idf[:], pattern=[[-1, 128]], compare_op=ALU.not_equal,
                                              fill=1.0, base=0, channel_multiplier=1), reads=[idf], writes=[idf])
        op('dve', lambda v: v.tensor_copy(out=idb[:], in_=idf[:]), reads=[idf], writes=[idb])
        for j in range(4):
            op('dve', lambda v: v.tensor_copy(out=id4[:, j, :], in_=idf[:]), reads=[idf], writes=[id4])
        op('pool', lambda g: g.memset(zb[:], 0.0), writes=[zb])
        for i in range(17):
            op('pool', lambda g: g.memset(vaug[i][:], 1.0), writes=[vaug[i]])
        if stage <= 0:
            fw.barrier()
            return nc

        ses = ExitStack()
        with ses:
            NSTG = 3
            stg = [fw.sb("stg%d" % i, [128, 3072], F32, es=ses) for i in range(NSTG)]
            stgi = [0]

            def load_cast(dst_ap, dst_buf, src_ap, ncols):
                s = stg[stgi[0] % NSTG]
                ce = ['dve', 'pool', 'act'][stgi[0] % 3]
                stgi[0] += 1
                dma('sp' if stgi[0] % 2 == 0 else 'act', s[:, 0:ncols], src_ap, writes=[s])
                if ce == 'act':
                    op('act', lambda a: a.activation(out=dst_ap, in_=s[:, 0:ncols], func=AF.Copy), reads=[s], writes=[dst_buf])
                else:
                    op(ce, lambda v: v.tensor_copy(out=dst_ap, in_=s[:, 0:ncols]), reads=[s], writes=[dst_buf])

            cT = fw.sb("cT", [128, 8, 6], F32, es=ses)
            for s_ in range(6):
                dma('sp', cT[:, :, s_], call[s_, :].rearrange("(k p) -> p k", p=128), writes=[cT])
            sgc = fw.sb("sgc", [128, 8, 6], F32, es=ses)
            op('act', lambda a: a.activation(out=sgc[:], in_=cT[:], func=AF.Sigmoid), reads=[cT], writes=[sgc])
            op('dve', lambda v: v.tensor_tensor(out=cT[:], in0=cT[:], in1=sgc[:], op=ALU.mult), reads=[cT, sgc], writes=[cT])
            modps = PA + PB
            for k in range(8):
                s = stg[stgi[0] % NSTG]
                stgi[0] += 1
                dma('sp' if k % 2 == 0 else 'act', s[:, :], w_mod[k * 128:(k + 1) * 128, :], writes=[s])
                for blk in range(6):
                    op('pe', lambda t: t.matmul(modps[blk][0:6, :], lhsT=cT[:, k, :], rhs=s[:, blk * 512:(blk + 1) * 512],
                                                 start=(k == 0), stop=(k == 7)), reads=[cT, s], writes=[modps[blk]], inc=(k == 7 or blk == 5))
            modrow = fw.sb("modrow", [6, 3 * D], F32, es=ses)
            dma('sp', modrow[:, :], b_mod[0:1, :].to_broadcast([6, 3 * D]), writes=[modrow])
            for blk in range(6):
                op('dve', lambda v: v.tensor_tensor(out=modrow[:, blk * 512:(blk + 1) * 512], in0=modps[blk][0:6, :],
                                                     in1=modrow[:, blk * 512:(blk + 1) * 512], op=ALU.add),
                   reads=[modps[blk], modrow], writes=[modrow])
            scr_tok = Tok("modscr")
            dma('sp', modscr[:, :], modrow[:, :], reads=[modrow], writes=[scr_tok])
            modT = fw.sb("modT", [128, 6, 24], F32, es=ses)
            for s_ in range(6):
                dma('sp', modT[:, s_, :], modscr[s_, :].rearrange("(j p) -> p j", p=128), reads=[scr_tok], writes=[modT])
            gnT = fw.sb("gnT", [128, 8], F32, es=ses)
            dma('sp', gnT[:, :], g_norm[0, :].rearrange("(j p) -> p j", p=128), writes=[gnT])
            for s_ in range(6):
                op('dve', lambda v: v.scalar_tensor_tensor(out=Amod[:, s_, :], in0=modT[:, s_, 8:16], scalar=1.0, in1=gnT[:, :],
                                                            op0=ALU.add, op1=ALU.mult), reads=[modT, gnT], writes=[Amod])
                op('dve', lambda v: v.tensor_copy(out=Bmod[:, s_, :], in_=modT[:, s_, 0:8]), reads=[modT], writes=[Bmod])
            dma('sp', gfB[:, :], g_final[0:1, :].to_broadcast([128, D]), writes=[gfB])
            if stage <= 1:
                fw.barrier()
                return nc

            for k in range(8):
                load_cast(win[:, k, :], win, w_in[k * 128:(k + 1) * 128, 0:TOKW], TOKW)
            wtok = Tok("wsc")
            wtmp = [fw.sb("wtmp%d" % i, [128, 1792], BF16, es=ses) for i in range(4)]
            for k in range(8):
                for half in range(2):
                    c0 = TOKW + half * 1792
                    t_ = wtmp[(2 * k + half) % 4]
                    load_cast(t_[:, 0:1792], t_, w_in[k * 128:(k + 1) * 128, c0:c0 + 1792], 1792)
                    dma('act', wsc[half * 7:(half + 1) * 7, :, k * 256:(k + 1) * 256].rearrange("a p c -> p a c"),
                        t_[:, 0:1792].rearrange("p (a c) -> p a c", c=256), reads=[t_], writes=[wtok])
            for (wsb, wsrc, nk) in ((wglu, w_glu, 4), (wap, w_ap, 4), (wsp, w_sp, 4), (wout, w_out, 8)):
                for k in range(nk):
                    load_cast(wsb[:, k, :], wsb, wsrc[k * 128:(k + 1) * 128, :], D)
            if stage <= 2:
                fw.barrier()
                return nc

            posf = fw.sb("posf", [128, 17], F32, es=ses)
            op('pool', lambda g: g.iota(posf[:, 0:16], pattern=[[128, 16]], base=0, channel_multiplier=1,
                                        allow_small_or_imprecise_dtypes=True), writes=[posf])
            op('pool', lambda g: g.iota(posf[:, 16:17], pattern=[[0, 1]], base=PAST, channel_multiplier=1,
                                        allow_small_or_imprecise_dtypes=True), reads=[posf], writes=[posf])
            ang = fw.sb("ang", [128, 17, 8], F32, es=ses)
            for j in range(8):
                invj = float(500000.0 ** (-j * 2.0 / 16.0))
                op('dve', lambda v: v.tensor_scalar(out=ang[:, :, j], in0=posf[:, :], scalar1=invj, scalar2=None, op0=ALU.mult),
                   reads=[posf], writes=[ang])
            rr = fw.sb("rr", [128, 144], F32, es=ses)
            ri_ = fw.sb("ri_", [128, 144], I32, es=ses)
            rm = fw.sb("rm", [128, 144], F32, es=ses)

            def sin_of(dst_ap, dst_buf, src_ap, src_buf, n, shift):
                r = rr[:, 0:n]
                op('dve', lambda v: v.tensor_scalar(out=r, in0=src_ap, scalar1=shift, scalar2=1.0 / TWO_PI, op0=ALU.add, op1=ALU.mult),
                   reads=[src_buf], writes=[rr])
                op('dve', lambda v: v.tensor_copy(out=ri_[:, 0:n], in_=r), reads=[rr], writes=[ri_])
                op('dve', lambda v: v.tensor_copy(out=rm[:, 0:n], in_=ri_[:, 0:n]), reads=[ri_], writes=[rm])
                op('dve', lambda v: v.tensor_tensor(out=r, in0=r, in1=rm[:, 0:n], op=ALU.subtract), reads=[rr, rm], writes=[rr])
                op('dve', lambda v: v.tensor_scalar(out=rm[:, 0:n], in0=r, scalar1=0.5, scalar2=None, op0=ALU.is_gt), reads=[rr], writes=[rm])
                op('dve', lambda v: v.tensor_tensor(out=r, in0=r, in1=rm[:, 0:n], op=ALU.subtract), reads=[rr, rm], writes=[rr])
                op('dve', lambda v: v.tensor_scalar(out=rm[:, 0:n], in0=r, scalar1=-0.5, scalar2=None, op0=ALU.is_lt), reads=[rr], writes=[rm])
                op('dve', lambda v: v.tensor_tensor(out=r, in0=r, in1=rm[:, 0:n], op=ALU.add), reads=[rr, rm], writes=[rr])
                op('dve', lambda v: v.tensor_scalar(out=r, in0=r, scalar1=0.4999999, scalar2=-0.4999999, op0=ALU.min, op1=ALU.max),
                   reads=[rr], writes=[rr])
                op('act', lambda a: a.activation(out=dst_ap, in_=r, func=AF.Sin, scale=TWO_PI), reads=[rr], writes=[dst_buf])

            sin_of(ropeS[:].rearrange("p a b -> p (a b)"), ropeS, ang[:].rearrange("p a b -> p (a b)"), ang, 136, 0.0)
            sin_of(ropeC[:].rearrange("p a b -> p (a b)"), ropeC, ang[:].rearrange("p a b -> p (a b)"), ang, 136, math.pi / 2)
            if stage <= 3:
                fw.barrier()
                return nc

            lr = fw.sb("lr", [128, 16], F32, es=ses); li = fw.sb("li", [128, 16], F32, es=ses)
            dtn = fw.sb("dtn", [128, 16], F32, es=ses)
            dma('sp', lr[:, :], lam_re[0, :].rearrange("(j p) -> p j", p=128), writes=[lr])
            dma('sp', li[:, :], lam_im[0, :].rearrange("(j p) -> p j", p=128), writes=[li])
            ldv = log_dt[0, :].rearrange("(j two) -> two j", two=2)
            dma('sp', dtn[0:64, :], ldv[0:1, :].to_broadcast([64, 16]), writes=[dtn])
            dma('sp', dtn[64:128, :], ldv[1:2, :].to_broadcast([64, 16]), writes=[dtn])
            op('act', lambda a: a.activation(out=dtn[:], in_=dtn[:], func=AF.Exp), reads=[dtn], writes=[dtn])
            angs = fw.sb("angs", [128, 16], F32, es=ses)
            op('dve', lambda v: v.tensor_tensor(out=angs[:], in0=li[:], in1=dtn[:], op=ALU.mult), reads=[li, dtn], writes=[angs])
            op('dve', lambda v: v.tensor_tensor(out=rho[:], in0=lr[:], in1=dtn[:], op=ALU.mult), reads=[lr, dtn], writes=[rho])
            op('act', lambda a: a.activation(out=rho[:], in_=rho[:], func=AF.Exp), reads=[rho], writes=[rho])
            c1 = fw.sb("c1", [128, 16], F32, es=ses); s1 = fw.sb("s1", [128, 16], F32, es=ses)
            sin_of(s1[:], s1, angs[:], angs, 16, 0.0)
            sin_of(c1[:], c1, angs[:], angs, 16, math.pi / 2)
            ar = fw.sb("ar", [128, 16], F32, es=ses); ai = fw.sb("ai", [128, 16], F32, es=ses)
            op('dve', lambda v: v.tensor_tensor(out=ar[:], in0=rho[:], in1=c1[:], op=ALU.mult), reads=[rho, c1], writes=[ar])
            op('dve', lambda v: v.tensor_tensor(out=ai[:], in0=rho[:], in1=s1[:], op=ALU.mult), reads=[rho, s1], writes=[ai])
            den = fw.sb("den", [128, 16], F32, es=ses); t1 = fw.sb("t1", [128, 16], F32, es=ses); t2 = fw.sb("t2", [128, 16], F32, es=ses)
            zr = fw.sb("zr", [128, 16], F32, es=ses); zi = fw.sb("zi", [128, 16], F32, es=ses); am1 = fw.sb("am1", [128, 16], F32, es=ses)
            op('dve', lambda v: v.tensor_tensor(out=den[:], in0=lr[:], in1=lr[:], op=ALU.mult), reads=[lr], writes=[den])
            op('dve', lambda v: v.tensor_tensor(out=t1[:], in0=li[:], in1=li[:], op=ALU.mult), reads=[li], writes=[t1])
            op('dve', lambda v: v.tensor_tensor(out=den[:], in0=den[:], in1=t1[:], op=ALU.add), reads=[den, t1], writes=[den])
            op('dve', lambda v: v.reciprocal(out=den[:], in_=den[:]), reads=[den], writes=[den])
            op('dve', lambda v: v.tensor_scalar(out=am1[:], in0=ar[:], scalar1=-1.0, scalar2=None, op0=ALU.add), reads=[ar], writes=[am1])
            op('dve', lambda v: v.tensor_tensor(out=t1[:], in0=am1[:], in1=lr[:], op=ALU.mult), reads=[am1, lr], writes=[t1])
            op('dve', lambda v: v.tensor_tensor(out=t2[:], in0=ai[:], in1=li[:], op=ALU.mult), reads=[ai, li], writes=[t2])
            op('dve', lambda v: v.tensor_tensor(out=t1[:], in0=t1[:], in1=t2[:], op=ALU.add), reads=[t1, t2], writes=[t1])
            op('dve', lambda v: v.tensor_tensor(out=zr[:], in0=t1[:], in1=den[:], op=ALU.mult), reads=[t1, den], writes=[zr])
            op('dve', lambda v: v.tensor_tensor(out=t1[:], in0=ai[:], in1=lr[:], op=ALU.mult), reads=[ai, lr], writes=[t1])
            op('dve', lambda v: v.tensor_tensor(out=t2[:], in0=am1[:], in1=li[:], op=ALU.mult), reads=[am1, li], writes=[t2])
            op('dve', lambda v: v.tensor_tensor(out=t1[:], in0=t1[:], in1=t2[:], op=ALU.subtract), reads=[t1, t2], writes=[t1])
            op('dve', lambda v: v.tensor_tensor(out=zi[:], in0=t1[:], in1=den[:], op=ALU.mult), reads=[t1, den], writes=[zi])

            op('dve', lambda v: v.tensor_copy(out=cosT[:, :, 0], in_=c1[:]), reads=[c1], writes=[cosT])
            op('dve', lambda v: v.tensor_copy(out=sinT[:, :, 0], in_=s1[:]), reads=[s1], writes=[sinT])
            tA = fw.sb("tA", [128, 16, 32], F32, es=ses); tB = fw.sb("tB", [128, 16, 32], F32, es=ses)
            L = 1
            while L < 64:
                cl = cosT[:, :, L - 1:L].to_broadcast([128, 16, L])
                sl = sinT[:, :, L - 1:L].to_broadcast([128, 16, L])
                op('dve', lambda v: v.tensor_tensor(out=tA[:, :, 0:L], in0=cosT[:, :, 0:L], in1=cl, op=ALU.mult), reads=[cosT], writes=[tA])
                op('dve', lambda v: v.tensor_tensor(out=tB[:, :, 0:L], in0=sinT[:, :, 0:L], in1=sl, op=ALU.mult), reads=[sinT], writes=[tB])
                op('dve', lambda v: v.tensor_tensor(out=tA[:, :, 0:L], in0=tA[:, :, 0:L], in1=tB[:, :, 0:L], op=ALU.subtract), reads=[tA, tB], writes=[tA])
                op('dve', lambda v: v.tensor_tensor(out=tB[:, :, 0:L], in0=sinT[:, :, 0:L], in1=cl, op=ALU.mult), reads=[sinT, cosT], writes=[tB])
                op('dve', lambda v: v.tensor_copy(out=cosT[:, :, L:2 * L], in_=tA[:, :, 0:L]), reads=[tA], writes=[cosT])
                op('dve', lambda v: v.tensor_tensor(out=tA[:, :, 0:L], in0=cosT[:, :, 0:L], in1=sl, op=ALU.mult), reads=[cosT, sinT], writes=[tA])
                op('dve', lambda v: v.tensor_tensor(out=sinT[:, :, L:2 * L], in0=tA[:, :, 0:L], in1=tB[:, :, 0:L], op=ALU.add), reads=[tA, tB], writes=[sinT])
                L *= 2

            bn_r = fw.sb("bn_r", [128, 16, 16], F32, es=ses); bn_i = fw.sb("bn_i", [128, 16, 16], F32, es=ses)
            dma('sp', bn_r[:], b_re.rearrange("(j p) c -> p j c", p=128), writes=[bn_r])
            dma('sp', bn_i[:], b_im.rearrange("(j p) c -> p j c", p=128), writes=[bn_i])
            bb_r = fw.sb("bb_r", [128, 16, 16], F32, es=ses); bb_i = fw.sb("bb_i", [128, 16, 16], F32, es=ses)
            tq = fw.sb("tq", [128, 16, 16], F32, es=ses)
            zrb = zr[:, :].unsqueeze(2).to_broadcast([128, 16, 16])
            zib = zi[:, :].unsqueeze(2).to_broadcast([128, 16, 16])
            op('dve', lambda v: v.tensor_tensor(out=bb_r[:], in0=bn_r[:], in1=zrb, op=ALU.mult), reads=[bn_r, zr], writes=[bb_r])
            op('dve', lambda v: v.tensor_tensor(out=tq[:], in0=bn_i[:], in1=zib, op=ALU.mult), reads=[bn_i, zi], writes=[tq])
            op('dve', lambda v: v.tensor_tensor(out=bb_r[:], in0=bb_r[:], in1=tq[:], op=ALU.subtract), reads=[bb_r, tq], writes=[bb_r])
            op('dve', lambda v: v.tensor_tensor(out=bb_i[:], in0=bn_i[:], in1=zrb, op=ALU.mult), reads=[bn_i, zr], writes=[bb_i])
            op('dve', lambda v: v.tensor_tensor(out=tq[:], in0=bn_r[:], in1=zib, op=ALU.mult), reads=[bn_r, zi], writes=[tq])
            op('dve', lambda v: v.tensor_tensor(out=bb_i[:], in0=bb_i[:], in1=tq[:], op=ALU.add), reads=[bb_i, tq], writes=[bb_i])
            bexp = [fw.sb("bexp%d" % i, [128, 128], F32, es=ses) for i in range(2)]
            cnat_r = fw.sb("cnat_r", [128, 4, 64], F32, es=ses); cnat_i = fw.sb("cnat_i", [128, 4, 64], F32, es=ses)
            dma('sp', cnat_r[:], c_re.rearrange("(j p) n -> p j n", p=128), writes=[cnat_r])
            dma('sp', cnat_i[:], c_im.rearrange("(j p) n -> p j n", p=128), writes=[cnat_i])
            pid = fw.sb("pid", [128, 1], I32, es=ses)
            op('pool', lambda g: g.iota(pid[:, :], pattern=[[0, 1]], base=0, channel_multiplier=1), writes=[pid])
            op('dve', lambda v: v.tensor_scalar(out=pid[:], in0=pid[:], scalar1=4, scalar2=1, op0=ALU.arith_shift_right, op1=ALU.bitwise_and),
               reads=[pid], writes=[pid])
            mb = fw.sb("mb", [128, 2], F32, es=ses)
            op('dve', lambda v: v.tensor_copy(out=mb[:, 1:2], in_=pid[:]), reads=[pid], writes=[mb])
            op('dve', lambda v: v.tensor_scalar(out=mb[:, 0:1], in0=mb[:, 1:2], scalar1=-1.0, scalar2=1.0, op0=ALU.mult, op1=ALU.add),
               reads=[mb], writes=[mb])
            xi_ = [0]
            for pc in range(4):
                for q in range(4):
                    pt = 4 * pc + q
                    for ri, bb in enumerate((bb_r, bb_i)):
                        be = bexp[xi_[0] % 2]
                        xi_[0] += 1
                        op('pool', lambda g: g.memset(be[:], 0.0), writes=[be])
                        op('dve', lambda v: v.tensor_copy(out=be[0:64, 32 * q:32 * q + 16], in_=bb[0:64, pt, :]), reads=[bb], writes=[be])
                        op('dve', lambda v: v.tensor_copy(out=be[64:128, 32 * q + 16:32 * q + 32], in_=bb[64:128, pt, :]), reads=[bb, be], writes=[be])
                        pb_ = nextB()
                        op('pe', lambda t: t.transpose(out=pb_[:, 0:128], in_=be[:], identity=idf[:]), reads=[be, idf], writes=[pb_])
                        op('act', lambda a: a.activation(out=wbz[:, pc, q, ri, :], in_=pb_[:, 0:128], func=AF.Copy), reads=[pb_], writes=[wbz])
                    for ri, cn in enumerate((cnat_r, cnat_i)):
                        be = bexp[xi_[0] % 2]
                        xi_[0] += 1
                        op('pool', lambda g: g.memset(be[:], 0.0), writes=[be])
                        sl_ = slice(32 * q, 32 * q + 32)
                        sgn = 1.0 if ri == 0 else -1.0
                        op('dve', lambda v: v.tensor_scalar(out=be[sl_, 0:64], in0=cn[sl_, pc, :], scalar1=mb[sl_, 0:1], scalar2=sgn,
                                                             op0=ALU.mult, op1=ALU.mult), reads=[cn, mb], writes=[be])
                        op('dve', lambda v: v.tensor_scalar(out=be[sl_, 64:128], in0=cn[sl_, pc, :], scalar1=mb[sl_, 1:2], scalar2=sgn,
                                                             op0=ALU.mult, op1=ALU.mult), reads=[cn, mb, be], writes=[be])
                        pb_ = nextB()
                        op('pe', lambda t: t.transpose(out=pb_[:, 0:128], in_=be[:], identity=idf[:]), reads=[be, idf], writes=[pb_])
                        op('act', lambda a: a.activation(out=cz[:, pc, q, ri, :], in_=pb_[:, 0:128], func=AF.Copy), reads=[pb_], writes=[cz])
            dma('sp', dsk[:, :], d_skip[0, :].rearrange("(j p) -> p j", p=128), writes=[dsk])
            fw.barrier()
            if stage <= 4:
                return nc
        xbuf = [fw.sb("xbuf%d" % i, [128, D], F32) for i in range(1)]
        hT = fw.sb("hT", [128, 8, 128], BF16)
        tokm = fw.sb("tokm", [128, TOKW], F32)
        rt = [fw.sb("rt%d" % i, [128, 19, 8], F32) for i in range(4)]
        qb = fw.sb("qb", [128, 512], BF16); qib = fw.sb("qib", [128, 512], BF16)
        kb2 = fw.sb("kb2", [128, 128], BF16); kib = fw.sb("kib", [128, 2, 64], BF16)
        qT2 = [fw.sb("qT%d" % i, [128, 4, 128], BF16) for i in range(2)]; qiT2 = [fw.sb("qiT%d" % i, [128, 4, 128], BF16) for i in range(2)]
        Dm2 = [fw.sb("Dm%d" % i, [128, 8, 128], BF16) for i in range(2)]
        aw2 = [fw.sb("aw%d" % i, [128, 8], F32) for i in range(2)]; sg82 = [fw.sb("sg8%d" % i, [128, 8], F32) for i in range(2)]
        relu = [fw.sb("relu%d" % i, [128, 512], BF16) for i in range(2)]
        score = fw.sb("score", [128, 2176], F32)
        negm = fw.sb("negm", [128, 2176], BF16)
        pT = [fw.sb("pT%d" % i, [128, 4, 128], BF16) for i in range(3)]
        osb = fw.sb("osb", [128, 8, 65], F32); rden = fw.sb("rden", [128, 8], F32)
        og = fw.sb("og", [128, 512], BF16)
        ogT = fw.sb("ogT", [128, 4, 128], BF16)
        sgaT2 = [fw.sb("sgaT%d" % i, [128, 4, 128], BF16) for i in range(2)]
        uf = fw.sb("uf", [128, 4, 128], F32); ub = fw.sb("ub", [128, 4, 128], BF16)
        sgs2 = [fw.sb("sgs%d" % i, [128, 4, 128], BF16) for i in range(2)]
        sma2 = [fw.sb("sma%d" % i, [128, 8, 128], BF16) for i in range(2)]; smb2 = [fw.sb("smb%d" % i, [128, 8, 128], BF16) for i in range(2)]
        mm = [fw.sb("mm%d" % i, [128, 4, 64], F32) for i in range(2)]
        xr = fw.sb("xr", [128, 4, 64], F32); xi = fw.sb("xi", [128, 4, 64], F32)
        gr2 = [fw.sb("gr%d" % i, [128, 4, 64], F32) for i in range(2)]; gi2 = [fw.sb("gi%d" % i, [128, 4, 64], F32) for i in range(2)]
        hrb = [fw.sb("hrb%d" % i, [128, 4, 64], BF16) for i in range(2)]
        hib = [fw.sb("hib%d" % i, [128, 4, 64], BF16) for i in range(2)]
        cty = [fw.sb("cty%d" % i, [128, 4], F32) for i in range(6)]
        ysum = fw.sb("ysum", [128, 128], F32); yt1 = fw.sb("yt1", [128, 128], F32); yt2 = fw.sb("yt2", [128, 128], F32)
        ygb2 = [fw.sb("ygb%d" % i, [128, 4, 128], BF16) for i in range(2)]
        sgl = fw.sb("sgl", [128, 128], F32); gt1 = fw.sb("gt1", [128, 128], F32)
        yglu = fw.sb("yglu", [128, 4, 128], BF16)
        bat = fw.sb("bat", [128, 128], F32); mgt = fw.sb("mgt", [128, 128], F32)
        merged = fw.sb("merged", [128, 8, 128], BF16)
        res = fw.sb("res", [128, D], F32); ftmp = fw.sb("ftmp", [128, D], F32)
        st = [fw.sb("st%d" % i, [128, 1], F32) for i in range(4)]
        blo = fw.sb("blo", [128, 1], F32); bmid = fw.sb("bmid", [128, 1], F32); bcnt = fw.sb("bcnt", [128, 1], F32)
        bhalf = fw.sb("bhalf", [128, NITER + 1], F32); btmp = fw.sb("btmp", [128, 1], F32)
        btab = fw.sb("btab", [128, NITER + 1], F32)
        one_c = fw.sb("one_c", [128, 3], F32)
        for ci_, cv_ in enumerate((1.0, 40.0, 63.830764864229232)):
            op('pool', lambda g: g.memset(one_c[:, ci_:ci_ + 1], cv_), reads=[one_c], writes=[one_c])
        for it in range(NITER + 1):
            op('pool', lambda g: g.memset(btab[:, it:it + 1], float(0.5 ** (it + 1))), reads=[btab], writes=[btab])
        cst = fw.sb("cst", [128, 128], F32)
        wri = [0]

        class WView(Buf):
            def __init__(self, base):
                self.base = base
                self.ap3 = base.t[:].bitcast(BF16).rearrange("p (k c) -> p k c", c=256)

            def __getitem__(self, idx):
                return self.ap3[idx]

        wslots = [(wring[0], wring[0]), (wring[1], wring[1]), (res, WView(res)), (ftmp, WView(ftmp)), (xbuf[0], WView(xbuf[0]))]

        def stream_piece(pi):
            tok_, wb = wslots[wri[0] % 5]
            wri[0] += 1
            dma('sp', wb[:].rearrange("p k c -> p (k c)"), wsc[pi, :, :], writes=[tok_])
            return tok_, wb

        def ssm_A(pc, t0, i):
            nt = 64
            if ssm_mode['pc']:
                bur = PC[0]; bui = PC[0]; bo = 256
            else:
                bur = PA[(2 * i) % 3]; bui = PA[(2 * i + 1) % 3]; bo = 0
            gr = gr2[i % 2]; gi = gi2[i % 2]
            for q in range(4):
                op('pe', lambda t: t.matmul(bur[:, q * 64:q * 64 + nt], lhsT=wbz[:, pc, q, 0, :], rhs=ub[:, pc, t0:t0 + nt], start=True, stop=True),
                   reads=[wbz, ub], writes=[bur], inc=False)
                op('pe', lambda t: t.matmul(bui[:, bo + q * 64:bo + q * 64 + nt], lhsT=wbz[:, pc, q, 1, :], rhs=ub[:, pc, t0:t0 + nt], start=True, stop=True),
                   reads=[wbz, ub], writes=[bui], inc=(q == 3))
            bur3 = bur[:, 0:256].rearrange("p (q t) -> p q t", t=64)
            bui3 = bui[:, bo:bo + 256].rearrange("p (q t) -> p q t", t=64)
            ct = cosT[:, 4 * pc:4 * pc + 4, 0:nt]
            sn = sinT[:, 4 * pc:4 * pc + 4, 0:nt]
            op('dve', lambda v: v.tensor_tensor(out=mm[0][:], in0=bur3, in1=ct, op=ALU.mult), reads=[bur, cosT], writes=[mm[0]])
            op('dve', lambda v: v.tensor_tensor(out=mm[1][:], in0=bui3, in1=sn, op=ALU.mult), reads=[bui, sinT], writes=[mm[1]])
            op('dve', lambda v: v.tensor_tensor(out=xr[:], in0=mm[0][:], in1=mm[1][:], op=ALU.add), reads=[mm[0], mm[1]], writes=[xr])
            op('dve', lambda v: v.tensor_tensor(out=mm[0][:], in0=bui3, in1=ct, op=ALU.mult), reads=[bui, cosT, mm[0]], writes=[mm[0]])
            op('dve', lambda v: v.tensor_tensor(out=mm[1][:], in0=bur3, in1=sn, op=ALU.mult), reads=[bur, sinT, mm[1]], writes=[mm[1]])
            op('dve', lambda v: v.tensor_tensor(out=xi[:], in0=mm[0][:], in1=mm[1][:], op=ALU.subtract), reads=[mm[0], mm[1]], writes=[xi])
            for q in range(4):
                j = 4 * pc + q
                rb = rho[:, j:j + 1].to_broadcast([128, nt])
                op('dve', lambda v: v.tensor_tensor_scan(out=gr[:, q, :], data0=rb, data1=xr[:, q, :], initial=hprev[:, 0, j:j + 1], op0=ALU.mult, op1=ALU.add),
                   reads=[xr, rho, hp_tok[pc]], writes=[gr])
                op('dve', lambda v: v.tensor_tensor_scan(out=gi[:, q, :], data0=rb, data1=xi[:, q, :], initial=hprev[:, 1, j:j + 1], op0=ALU.mult, op1=ALU.add),
                   reads=[xi, rho, hp_tok[pc]], writes=[gi])

        def ssm_B(pc, i):
            nt = 64
            gr = gr2[i % 2]; gi = gi2[i % 2]
            hr_ = hrb[i % 2]; hi_ = hib[i % 2]
            ct = cosT[:, 4 * pc:4 * pc + 4, 0:nt]
            sn = sinT[:, 4 * pc:4 * pc + 4, 0:nt]
            cl = cosT[:, 4 * pc:4 * pc + 4, nt - 1]
            sl = sinT[:, 4 * pc:4 * pc + 4, nt - 1]
            op('pool', lambda v: v.tensor_tensor(out=cty[0][:], in0=gr[:, :, nt - 1], in1=cl, op=ALU.mult), reads=[gr, cosT], writes=[cty[0]])
            op('pool', lambda v: v.tensor_tensor(out=cty[1][:], in0=gi[:, :, nt - 1], in1=sl, op=ALU.mult), reads=[gi, sinT], writes=[cty[1]])
            op('pool', lambda v: v.tensor_tensor(out=cty[2][:], in0=gi[:, :, nt - 1], in1=cl, op=ALU.mult), reads=[gi, cosT], writes=[cty[2]])
            op('pool', lambda v: v.tensor_tensor(out=cty[3][:], in0=gr[:, :, nt - 1], in1=sl, op=ALU.mult), reads=[gr, sinT], writes=[cty[3]])
            op('pool', lambda v: v.tensor_tensor(out=hprev[:, 0, 4 * pc:4 * pc + 4], in0=cty[0][:], in1=cty[1][:], op=ALU.subtract),
               reads=[cty[0], cty[1]], writes=[hp_tok[pc]])
            op('pool', lambda v: v.tensor_tensor(out=hprev[:, 1, 4 * pc:4 * pc + 4], in0=cty[2][:], in1=cty[3][:], op=ALU.add),
               reads=[cty[2], cty[3]], writes=[hp_tok[pc]])
            d = [rt[k_][:].rearrange("p a b -> p (a b)").bitcast(BF16)[:, 0:256].rearrange("p (q t) -> p q t", t=64) for k_ in range(4)]
            op('pool', lambda v: v.tensor_tensor(out=d[0], in0=gr[:], in1=ct, op=ALU.mult), reads=[gr, cosT], writes=[rt[0]])
            op('pool', lambda v: v.tensor_tensor(out=d[1], in0=gi[:], in1=sn, op=ALU.mult), reads=[gi, sinT], writes=[rt[1]])
            op('pool', lambda v: v.tensor_tensor(out=hr_[:], in0=d[0], in1=d[1], op=ALU.subtract), reads=[rt[0], rt[1]], writes=[hr_])
            op('pool', lambda v: v.tensor_tensor(out=d[2], in0=gi[:], in1=ct, op=ALU.mult), reads=[gi, cosT], writes=[rt[2]])
            op('pool', lambda v: v.tensor_tensor(out=d[3], in0=gr[:], in1=sn, op=ALU.mult), reads=[gr, sinT], writes=[rt[3]])
            op('pool', lambda v: v.tensor_tensor(out=hi_[:], in0=d[2], in1=d[3], op=ALU.add), reads=[rt[2], rt[3]], writes=[hi_])
            return hr_, hi_

        ssm_cnt = [0]
        ssm_mode = {'pc': False}

        def front(c, pre=None):
            s_, is_prompt, b_, qt, xsrc, outs, slot = c
            tp = 128 if is_prompt else 64
            tok0 = qt * 128 if is_prompt else 0
            kt_new = qt if is_prompt else 8
            kc0 = kt_new * 128
            nk = kc0 + tp
            y_out, k_out, v_out, ki_out = outs
            xb = xbuf[0]
            qT = qT2[slot]; qiT = qiT2[slot]; Dm = Dm2[slot]; aw = aw2[slot]; sg8 = sg82[slot]
            sgaT = sgaT2[slot]; sgs = sgs2[slot]; sma = sma2[slot]; smb = smb2[slot]; ygb = ygb2[slot]
            if pre is not None:
                pre()
            dma('sp', xb[0:tp, :], xsrc[tok0:tok0 + tp, :], writes=[xb])
            op('act', lambda a: a.activation(out=ftmp[0:tp, :], in_=xb[0:tp, :], func=AF.Square, accum_out=st[0][0:tp, :]), reads=[xb], writes=[ftmp, st[0]])
            op('dve', lambda v: v.tensor_scalar(out=st[1][0:tp, :], in0=st[0][0:tp, :], scalar1=1.0 / D, scalar2=1e-6, op0=ALU.mult, op1=ALU.add),
               reads=[st[0]], writes=[st[1]])
            op('act', lambda a: a.activation(out=st[1][0:tp, :], in_=st[1][0:tp, :], func=AF.Sqrt), reads=[st[1]], writes=[st[1]])
            op('dve', lambda v: v.reciprocal(out=st[1][0:tp, :], in_=st[1][0:tp, :]), reads=[st[1]], writes=[st[1]])
            op('act', lambda a: a.activation(out=xb[0:tp, :], in_=xb[0:tp, :], func=AF.Identity, scale=st[1][0:tp, 0:1]), reads=[xb, st[1]], writes=[xb])
            for j in range(8):
                pa = nextA()
                op('pe', lambda t: t.transpose(out=pa[:, 0:tp], in_=xb[0:tp, j * 128:(j + 1) * 128], identity=idf[0:tp, 0:tp]), reads=[xb, idf], writes=[pa])
                op('act', lambda a: a.activation(out=hT[:, j, 0:tp], in_=pa[:, 0:tp], func=AF.Identity, scale=Amod[:, s_, j:j + 1], bias=Bmod[:, s_, j:j + 1]),
                   reads=[pa, Amod, Bmod], writes=[hT])
            yield 'x'
            blks = [(0, 512), (512, 1024), (1024, TOKW)]
            for (c0, c1) in blks:
                pa = nextA()
                for k in range(8):
                    op('pe', lambda t: t.matmul(pa[0:tp, 0:c1 - c0], lhsT=hT[:, k, 0:tp], rhs=win[:, k, c0:c1], start=(k == 0), stop=(k == 7)),
                       reads=[hT, win], writes=[pa], inc=(k == 7))
                op('act', lambda a: a.activation(out=tokm[0:tp, c0:c1], in_=pa[0:tp, 0:c1 - c0], func=AF.Copy), reads=[pa], writes=[tokm])
            yield 'tok'
            import os
            _dbg = int(os.environ.get("DBG_FM", "9"))
            for pi in range(NPIECE):
                wtok, wb = stream_piece(pi)
                if _dbg <= 0:
                    continue
                for f in range(2):
                    ft = 2 * pi + f
                    pa = nextA()
                    for k in range(8):
                        op('pe', lambda t: t.matmul(pa[:, 0:tp], lhsT=wb[:, k, f * 128:(f + 1) * 128], rhs=hT[:, k, 0:tp], start=(k == 0), stop=(k == 7)),
                           reads=[wtok, hT], writes=[pa], inc=(k == 7))
                    if _dbg <= 1:
                        continue
                    if _dbg <= 2:
                        op('act', lambda a: a.activation(out=sgaT[:, ft % 4, 0:tp], in_=pa[:, 0:tp], func=AF.Sigmoid), reads=[pa], writes=[sgaT])
                        continue
                    _grp = 0 if ft < 4 else 1 if ft < 8 else 2 if ft < 12 else 3 if ft < 20 else 4
                    if not (int(os.environ.get("DBG_FT", "31")) >> _grp) & 1:
                        continue
                    if ft < 4:
                        op('act', lambda a: a.activation(out=sgl[:, 0:tp], in_=pa[:, 0:tp], func=AF.Sigmoid), reads=[pa], writes=[sgl])
                        op('act', lambda a: a.activation(out=gt1[:, 0:tp], in_=pa[:, 0:tp], func=AF.Copy), reads=[pa], writes=[gt1])
                        op('pool', lambda v: v.tensor_tensor(out=sgaT[:, ft, 0:tp], in0=gt1[:, 0:tp], in1=sgl[:, 0:tp], op=ALU.mult), reads=[gt1, sgl], writes=[sgaT])
                    elif ft < 8:
                        op('act', lambda a: a.activation(out=uf[:, ft - 4, 0:tp], in_=pa[:, 0:tp], func=AF.Copy), reads=[pa], writes=[uf])
                        op('pool', lambda v: v.tensor_copy(out=ub[:, ft - 4, 0:tp], in_=uf[:, ft - 4, 0:tp]), reads=[uf], writes=[ub])
                    elif ft < 12:
                        op('act', lambda a: a.activation(out=sgl[:, 0:tp], in_=pa[:, 0:tp], func=AF.Sigmoid), reads=[pa], writes=[sgl])
                        op('act', lambda a: a.activation(out=gt1[:, 0:tp], in_=pa[:, 0:tp], func=AF.Copy), reads=[pa], writes=[gt1])
                        op('pool', lambda v: v.tensor_tensor(out=sgs[:, ft - 8, 0:tp], in0=gt1[:, 0:tp], in1=sgl[:, 0:tp], op=ALU.mult), reads=[gt1, sgl], writes=[sgs])
                    elif ft < 20:
                        op('act', lambda a: a.activation(out=sma[:, ft - 12, 0:tp], in_=pa[:, 0:tp], func=AF.Sigmoid), reads=[pa], writes=[sma])
                    else:
                        op('act', lambda a: a.activation(out=smb[:, ft - 20, 0:tp], in_=pa[:, 0:tp], func=AF.Sigmoid), reads=[pa], writes=[smb])
            yield 'x'
            ci = qt if is_prompt else 16
            for (b0, nb) in ((0, 10), (12, 9)):
                blk = tokm[0:tp, 0:1344].rearrange("p (b d) -> p b d", d=64)
                x1 = blk[:, b0:b0 + nb, 0:8]
                x2 = blk[:, b0:b0 + nb, 8:16]
                cb = ropeC[0:tp, ci:ci + 1, :].to_broadcast([tp, nb, 8])
                sbb = ropeS[0:tp, ci:ci + 1, :].to_broadcast([tp, nb, 8])
                r_ = [x_[0:tp, 0:nb, :] for x_ in rt]
                op('pool', lambda v: v.tensor_tensor(out=r_[0], in0=x1, in1=cb, op=ALU.mult), reads=[tokm, ropeC], writes=[rt[0]])
                op('pool', lambda v: v.tensor_tensor(out=r_[1], in0=x2, in1=sbb, op=ALU.mult), reads=[tokm, ropeS], writes=[rt[1]])
                op('pool', lambda v: v.tensor_tensor(out=r_[2], in0=x2, in1=cb, op=ALU.mult), reads=[tokm, ropeC], writes=[rt[2]])
                op('pool', lambda v: v.tensor_tensor(out=r_[3], in0=x1, in1=sbb, op=ALU.mult), reads=[tokm, ropeS], writes=[rt[3]])
                op('pool', lambda v: v.tensor_tensor(out=x1, in0=r_[0], in1=r_[1], op=ALU.subtract), reads=[rt[0], rt[1], tokm], writes=[tokm])
                op('pool', lambda v: v.tensor_tensor(out=x2, in0=r_[2], in1=r_[3], op=ALU.add), reads=[rt[2], rt[3], tokm], writes=[tokm])
            yield 'x'
            dma('act', k_out[tok0:tok0 + tp, :], tokm[0:tp, 512:640], reads=[tokm])
            dma('act', v_out[tok0:tok0 + tp, :], tokm[0:tp, 640:768], reads=[tokm])
            dma('act', ki_out[tok0:tok0 + tp, :], tokm[0:tp, 1280:1344], reads=[tokm])
            yield 'x'
            op('act', lambda a: a.activation(out=qb[0:tp, :].rearrange("p (r g d) -> p g r d", r=4, g=2),
                                             in_=tokm[0:tp, 0:512].rearrange("p (g r d) -> p g r d", g=2, r=4), func=AF.Copy, scale=0.125),
               reads=[tokm], writes=[qb])
            op('act', lambda a: a.activation(out=qib[0:tp, :], in_=tokm[0:tp, 768:1280], func=AF.Copy), reads=[tokm], writes=[qib])
            op('pool', lambda v: v.tensor_copy(out=kb2[0:tp, :], in_=tokm[0:tp, 512:640]), reads=[tokm], writes=[kb2])
            op('pool', lambda v: v.tensor_copy(out=kib[0:tp, 0, :], in_=tokm[0:tp, 1280:1344]), reads=[tokm], writes=[kib])
            op('pool', lambda v: v.tensor_copy(out=kib[0:tp, 1, :], in_=tokm[0:tp, 1280:1344]), reads=[tokm, kib], writes=[kib])
            op('pool', lambda v: v.tensor_copy(out=vaug[kt_new][0:tp, :, 0:64], in_=tokm[0:tp, 640:768].rearrange("p (g d) -> p g d", d=64)),
               reads=[tokm], writes=[vaug[kt_new]])
            pb_ = nextB()
            pbb = pb_[:].bitcast(BF16)
            op('pe', lambda t: t.transpose(out=pbb[:, 0:tp], in_=kb2[0:tp, :], identity=idb[0:tp, 0:tp]), reads=[kb2, idb], writes=[pb_])
            op('act', lambda a: a.activation(out=kT[:, kc0:kc0 + tp], in_=pbb[:, 0:tp], func=AF.Copy), reads=[pb_], writes=[kT_tok[kt_new]])
            pb_ = nextB()
            pbb2 = pb_[:].bitcast(BF16)
            op('pe', lambda t: t.transpose(out=pbb2[:, 0:tp], in_=kib[0:tp, :, :].rearrange("p a d -> p (a d)"), identity=idb[0:tp, 0:tp]), reads=[kib, idb], writes=[pb_])
            op('act', lambda a: a.activation(out=kiT[:, kc0:kc0 + tp], in_=pbb2[:, 0:tp], func=AF.Copy), reads=[pb_], writes=[kiT_tok[kt_new]])
            for r in range(4):
                pb_ = nextB()
                pq = pb_[:].bitcast(BF16)
                op('pe', lambda t: t.transpose(out=pq[:, 0:tp], in_=qb[0:tp, r * 128:(r + 1) * 128], identity=idb[0:tp, 0:tp]), reads=[qb, idb], writes=[pb_])
                op('act', lambda a: a.activation(out=qT[:, r, 0:tp], in_=pq[:, 0:tp], func=AF.Copy), reads=[pb_], writes=[qT])
            for j in range(4):
                pb_ = nextB()
                pq2 = pb_[:].bitcast(BF16)
                op('pe', lambda t: t.transpose(out=pq2[:, 0:tp], in_=qib[0:tp, j * 128:(j + 1) * 128], identity=idb[0:tp, 0:tp]), reads=[qib, idb], writes=[pb_])
                op('act', lambda a: a.activation(out=qiT[:, j, 0:tp], in_=pq2[:, 0:tp], func=AF.Copy), reads=[pb_], writes=[qiT])
            yield 'x'
            op('act', lambda a: a.activation(out=aw[0:tp, :], in_=tokm[0:tp, 1344:1352], func=AF.Abs, scale=float(8 ** -0.5) * 0.125),
               reads=[tokm], writes=[aw])
            op('pool', lambda v: v.tensor_scalar(out=sg8[0:tp, :], in0=tokm[0:tp, 1344:1352], scalar1=0.0, scalar2=2.0, op0=ALU.is_ge, op1=ALU.mult),
               reads=[tokm], writes=[sg8])
            op('pool', lambda v: v.tensor_scalar(out=sg8[0:tp, :], in0=sg8[0:tp, :], scalar1=-1.0, scalar2=None, op0=ALU.add), reads=[sg8], writes=[sg8])
            for h in range(8):
                op('pool', lambda v: v.tensor_scalar(out=Dm[0:tp, h, 0:tp], in0=idb[0:tp, 0:tp], scalar1=sg8[0:tp, h:h + 1], scalar2=None, op0=ALU.mult),
                   reads=[idb, sg8], writes=[Dm])
            yield 'pre_ssm'
            nh = tp // 64
            units = [(hf, pc) for hf in range(nh) for pc in range(4)]
            i0 = ssm_cnt[0]
            def ssm_C(pc, t0, hr_, hi_):
                yps = PC[1] if ssm_mode['pc'] else PA[3]
                for q in range(4):
                    op('pe', lambda t: t.matmul(yps[:, 0:64], lhsT=cz[:, pc, q, 0, :], rhs=hr_[:, q, :], start=(q == 0), stop=False),
                       reads=[cz, hr_], writes=[yps], inc=False)
                    op('pe', lambda t: t.matmul(yps[:, 0:64], lhsT=cz[:, pc, q, 1, :], rhs=hi_[:, q, :], start=False, stop=(q == 3)),
                       reads=[cz, hi_], writes=[yps], inc=(q == 3))
                ts_ = slice(t0, t0 + 64)
                op('dve', lambda v: v.scalar_tensor_tensor(out=ysum[:, ts_], in0=uf[:, pc, ts_], scalar=dsk[:, pc:pc + 1], in1=yps[:, 0:64], op0=ALU.mult, op1=ALU.add),
                   reads=[uf, dsk, yps], writes=[ysum])
                op('dve', lambda v: v.tensor_tensor(out=yt1[:, ts_], in0=ysum[:, ts_], in1=ysum[:, ts_], op=ALU.mult), reads=[ysum], writes=[yt1])
                op('act', lambda a: a.activation(out=yt1[:, ts_], in_=yt1[:, ts_], func=AF.Identity, scale=0.044715, bias=one_c[:, 0:1]), reads=[yt1, one_c], writes=[yt1])
                op('dve', lambda v: v.tensor_tensor(out=yt1[:, ts_], in0=yt1[:, ts_], in1=ysum[:, ts_], op=ALU.mult), reads=[yt1, ysum], writes=[yt1])
                op('act', lambda a: a.activation(out=yt1[:, ts_], in_=yt1[:, ts_], func=AF.Relu, bias=one_c[:, 1:2]), reads=[yt1, one_c], writes=[yt1])
                op('act', lambda a: a.activation(out=yt2[:, ts_], in_=yt1[:, ts_], func=AF.Exp, scale=-1.5957691216057308, bias=one_c[:, 2:3]), reads=[yt1, one_c], writes=[yt2])
                op('dve', lambda v: v.tensor_scalar(out=yt2[:, ts_], in0=yt2[:, ts_], scalar1=1.0, scalar2=None, op0=ALU.add), reads=[yt2], writes=[yt2])
                op('dve', lambda v: v.reciprocal(out=yt2[:, ts_], in_=yt2[:, ts_]), reads=[yt2], writes=[yt2])
                op('dve', lambda v: v.tensor_tensor(out=ygb[:, pc, ts_], in0=ysum[:, ts_], in1=yt2[:, ts_], op=ALU.mult), reads=[ysum, yt2], writes=[ygb])

            ssm_A(units[0][1], units[0][0] * 64, i0)
            prevc = None
            for ui, (hf, pc) in enumerate(units):
                i = i0 + ui
                t0 = hf * 64
                if prevc is not None:
                    ssm_C(*prevc)
                if ui + 1 < len(units):
                    ssm_A(units[ui + 1][1], units[ui + 1][0] * 64, i + 1)
                hr_, hi_ = ssm_B(pc, i)
                prevc = (pc, t0, hr_, hi_)
                yield 'ssm'
            ssm_C(*prevc)
            ssm_cnt[0] += len(units)
            yield 'x'

        def idxgen(c):
            s_, is_prompt, b_, qt, xsrc, outs, slot = c
            tp = 128 if is_prompt else 64
            tok0 = qt * 128 if is_prompt else 0
            kt_new = qt if is_prompt else 8
            kc0 = kt_new * 128
            nk = kc0 + tp
            y_out, k_out, v_out, ki_out = outs
            xb = xbuf[0]
            qT = qT2[slot]; qiT = qiT2[slot]; Dm = Dm2[slot]; aw = aw2[slot]; sg8 = sg82[slot]
            sgaT = sgaT2[slot]; sgs = sgs2[slot]; sma = sma2[slot]; smb = smb2[slot]; ygb = ygb2[slot]
            nkb = (nk + 511) // 512
            items = [(kb_, h) for kb_ in range(nkb) for h in range(8)]

            def emit_s(i):
                kb_, h = items[i]
                c0 = kb_ * 512
                w_ = min(512, nk - c0)
                ktoks = kiT_tok[c0 // 128:(c0 + w_ + 127) // 128]
                pr = 64 * (h % 2)
                sp_ = PB[i % 2]
                rl = relu[i % 2]
                op('pe', lambda t: t.matmul(sp_[0:tp, 0:w_], lhsT=qiT[pr:pr + 64, h // 2, 0:tp], rhs=kiT[pr:pr + 64, c0:c0 + w_], start=True, stop=True),
                   reads=[qiT] + ktoks, writes=[sp_])
                if i % 2 == 0:
                    op('act', lambda a: a.activation(out=rl[0:tp, 0:w_], in_=sp_[0:tp, 0:w_], func=AF.Relu, scale=aw[0:tp, h:h + 1]), reads=[sp_, aw], writes=[rl])
                else:
                    op('dve', lambda v: v.tensor_scalar(out=rl[0:tp, 0:w_], in0=sp_[0:tp, 0:w_], scalar1=aw[0:tp, h:h + 1], scalar2=0.0, op0=ALU.mult, op1=ALU.max),
                       reads=[sp_, aw], writes=[rl])

            emit_s(0)
            for i in range(len(items)):
                if i + 1 < len(items):
                    emit_s(i + 1)
                kb_, h = items[i]
                c0 = kb_ * 512
                w_ = min(512, nk - c0)
                sc_ps = PA[kb_ % 4]
                rl = relu[i % 2]
                op('pe', lambda t: t.matmul(sc_ps[0:tp, 0:w_], lhsT=Dm[0:tp, h, 0:tp], rhs=rl[0:tp, 0:w_], start=(h == 0), stop=(h == 7)),
                   reads=[Dm, rl], writes=[sc_ps])
                if h == 7:
                    op('act', lambda a: a.activation(out=score[0:tp, c0:c0 + w_], in_=sc_ps[0:tp, 0:w_], func=AF.Copy), reads=[sc_ps], writes=[score])
                    yield 'idxblk'
            if is_prompt:
                op('pool', lambda g: g.memset(score[0:64, nk - 64:nk], -1e30), reads=[score], writes=[score])

        def back(c):
            s_, is_prompt, b_, qt, xsrc, outs, slot = c
            tp = 128 if is_prompt else 64
            tok0 = qt * 128 if is_prompt else 0
            kt_new = qt if is_prompt else 8
            kc0 = kt_new * 128
            nk = kc0 + tp
            y_out, k_out, v_out, ki_out = outs
            xb = xbuf[0]
            qT = qT2[slot]; qiT = qiT2[slot]; Dm = Dm2[slot]; aw = aw2[slot]; sg8 = sg82[slot]
            sgaT = sgaT2[slot]; sgs = sgs2[slot]; sma = sma2[slot]; smb = smb2[slot]; ygb = ygb2[slot]
            need_topk = (not is_prompt) or (qt >= 2)
            if need_topk:
                nlo = nk - 64 if is_prompt else nk
                op('dve', lambda v: v.tensor_reduce(out=st[2][0:tp, :], in_=score[0:tp, 0:nk], axis=AX.X, op=ALU.max), reads=[score], writes=[st[2]])
                op('dve', lambda v: v.tensor_reduce(out=blo[0:tp, :], in_=score[0:tp, 0:nlo], axis=AX.X, op=ALU.min), reads=[score], writes=[blo])
                op('dve', lambda v: v.tensor_tensor(out=st[3][0:tp, :], in0=st[2][0:tp, :], in1=blo[0:tp, :], op=ALU.subtract), reads=[st[2], blo], writes=[st[3]])
                op('dve', lambda v: v.tensor_scalar(out=bhalf[0:tp, :], in0=btab[0:tp, :], scalar1=st[3][0:tp, 0:1], scalar2=None, op0=ALU.mult),
                   reads=[st[3], btab], writes=[bhalf])
                op('dve', lambda v: v.tensor_tensor(out=bmid[0:tp, :], in0=blo[0:tp, :], in1=bhalf[0:tp, 0:1], op=ALU.add), reads=[blo, bhalf], writes=[bmid])
                for it in range(NITER):
                    op('dve', lambda v: v.tensor_scalar(out=negm[0:tp, 0:nk], in0=score[0:tp, 0:nk], scalar1=bmid[0:tp, 0:1], scalar2=0.0,
                                                         op0=ALU.is_gt, op1=ALU.add, accum_out=bcnt[0:tp, :]), reads=[score, bmid], writes=[negm, bcnt])
                    op('dve', lambda v: v.scalar_tensor_tensor(out=btmp[0:tp, :], in0=bcnt[0:tp, :], scalar=256.0, in1=bhalf[0:tp, it:it + 1],
                                                                op0=ALU.is_ge, op1=ALU.mult), reads=[bcnt, bhalf], writes=[btmp])
                    op('dve', lambda v: v.scalar_tensor_tensor(out=bmid[0:tp, :], in0=bmid[0:tp, :], scalar=bhalf[0:tp, it + 1:it + 2], in1=btmp[0:tp, :],
                                                                op0=ALU.subtract, op1=ALU.add), reads=[bmid, bhalf, btmp], writes=[bmid])
                op('dve', lambda v: v.tensor_tensor(out=blo[0:tp, :], in0=bmid[0:tp, :], in1=bhalf[0:tp, NITER:NITER + 1], op=ALU.subtract),
                   reads=[bmid, bhalf], writes=[blo])
                op('dve', lambda v: v.tensor_scalar(out=negm[0:tp, 0:nk], in0=score[0:tp, 0:nk], scalar1=blo[0:tp, 0:1], scalar2=NEG, op0=ALU.is_le, op1=ALU.mult),
                   reads=[score, blo], writes=[negm])
            else:
                op('dve', lambda v: v.tensor_scalar(out=negm[0:tp, 0:nk], in0=score[0:tp, 0:nk], scalar1=-1e29, scalar2=NEG, op0=ALU.is_le, op1=ALU.mult),
                   reads=[score], writes=[negm])
            yield 'bis'
            for g in range(2):
                op('pe', lambda t: t.matmul(PC[g][0:tp, 0:260], lhsT=zb[:, 0:tp], rhs=zb[:, 0:260], start=True, stop=True), reads=[zb], writes=[PC[g]])
            nkt = (nk + 127) // 128
            aitems = [(kt, g) for kt in range(nkt) for g in range(2)]

            def emit_qk(i):
                kt, g = aitems[i]
                k0 = kt * 128
                kw = min(128, nk - k0)
                lp = PB[i % 2]
                pt_ = pT[i % 3]
                op('pe', lambda t: t.matmul(lp[0:kw, 0:4 * tp], lhsT=kT[64 * g:64 * g + 64, k0:k0 + kw], rhs=qT[64 * g:64 * g + 64, :, 0:tp],
                                             start=True, stop=False), reads=[kT_tok[kt], qT], writes=[lp], inc=False)
                op('pe', lambda t: t.matmul(lp[0:kw, 0:4 * tp], lhsT=negm[:, k0:k0 + kw], rhs=id4[:, :, 0:tp], start=False, stop=True),
                   reads=[negm, id4], writes=[lp])
                op('act', lambda a: a.activation(out=pt_[0:kw, :, 0:tp], in_=lp[0:kw, 0:4 * tp].rearrange("p (r t) -> p r t", r=4), func=AF.Exp),
                   reads=[lp], writes=[pt_])

            emit_qk(0)
            for i in range(len(aitems)):
                if i + 1 < len(aitems):
                    emit_qk(i + 1)
                kt, g = aitems[i]
                k0 = kt * 128
                kw = min(128, nk - k0)
                pt_ = pT[i % 3]
                for r in range(4):
                    op('pe', lambda t: t.matmul(PC[g][0:tp, r * 65:(r + 1) * 65], lhsT=pt_[0:kw, r, 0:tp], rhs=vaug[kt][0:kw, g, :],
                                                 start=False, stop=True, skip_group_check=True), reads=[pt_, vaug[kt]], writes=[PC[g]], inc=(r == 3))
                if g == 1:
                    yield 'att'
            for g in range(2):
                op('act', lambda a: a.activation(out=osb[0:tp, 4 * g:4 * g + 4, :], in_=PC[g][0:tp, 0:260].rearrange("p (r d) -> p r d", d=65), func=AF.Copy),
                   reads=[PC[g]], writes=[osb])
            op('dve', lambda v: v.reciprocal(out=rden[0:tp, :], in_=osb[0:tp, :, 64]), reads=[osb], writes=[rden])
            op('dve', lambda v: v.tensor_tensor(out=og[0:tp, :].rearrange("p (h d) -> p h d", d=64), in0=osb[0:tp, :, 0:64],
                                                 in1=rden[0:tp, :].unsqueeze(2).to_broadcast([tp, 8, 64]), op=ALU.mult), reads=[osb, rden], writes=[og])
            yield 'att_done'
            for j in range(4):
                pb_ = nextB()
                pq3 = pb_[:].bitcast(BF16)
                op('pe', lambda t: t.transpose(out=pq3[:, 0:tp], in_=og[0:tp, j * 128:(j + 1) * 128], identity=idb[0:tp, 0:tp]), reads=[og, idb], writes=[pb_])
                op('dve', lambda v: v.tensor_tensor(out=ogT[:, j, 0:tp], in0=pq3[:, 0:tp], in1=sgaT[:, j, 0:tp], op=ALU.mult), reads=[pb_, sgaT], writes=[ogT])
            yield 'x'
            for of in range(4):
                pa1 = nextA(); pa2 = nextA()
                for k in range(4):
                    op('pe', lambda t: t.matmul(pa1[:, 0:tp], lhsT=wglu[:, k, of * 128:(of + 1) * 128], rhs=ygb[:, k, 0:tp], start=(k == 0), stop=(k == 3)),
                       reads=[wglu, ygb], writes=[pa1], inc=(k == 3))
                for k in range(4):
                    op('pe', lambda t: t.matmul(pa2[:, 0:tp], lhsT=wglu[:, k, (of + 4) * 128:(of + 5) * 128], rhs=ygb[:, k, 0:tp], start=(k == 0), stop=(k == 3)),
                       reads=[wglu, ygb], writes=[pa2], inc=(k == 3))
                op('act', lambda a: a.activation(out=sgl[:, 0:tp], in_=pa2[:, 0:tp], func=AF.Sigmoid), reads=[pa2], writes=[sgl])
                op('dve', lambda v: v.tensor_tensor(out=gt1[:, 0:tp], in0=pa1[:, 0:tp], in1=sgl[:, 0:tp], op=ALU.mult), reads=[pa1, sgl], writes=[gt1])
                op('dve', lambda v: v.tensor_tensor(out=yglu[:, of, 0:tp], in0=gt1[:, 0:tp], in1=sgs[:, of, 0:tp], op=ALU.mult), reads=[gt1, sgs], writes=[yglu])
            yield 'x'
            for of in range(8):
                pa1 = nextA(); pa2 = nextA()
                for k in range(4):
                    op('pe', lambda t: t.matmul(pa1[:, 0:tp], lhsT=wap[:, k, of * 128:(of + 1) * 128], rhs=ogT[:, k, 0:tp], start=(k == 0), stop=(k == 3)),
                       reads=[wap, ogT], writes=[pa1], inc=(k == 3))
                for k in range(4):
                    op('pe', lambda t: t.matmul(pa2[:, 0:tp], lhsT=wsp[:, k, of * 128:(of + 1) * 128], rhs=yglu[:, k, 0:tp], start=(k == 0), stop=(k == 3)),
                       reads=[wsp, yglu], writes=[pa2], inc=(k == 3))
                op('dve', lambda v: v.tensor_tensor(out=bat[:, 0:tp], in0=pa1[:, 0:tp], in1=sma[:, of, 0:tp], op=ALU.mult), reads=[pa1, sma], writes=[bat])
                op('dve', lambda v: v.tensor_tensor(out=mgt[:, 0:tp], in0=pa2[:, 0:tp], in1=smb[:, of, 0:tp], op=ALU.mult), reads=[pa2, smb], writes=[mgt])
                op('dve', lambda v: v.tensor_tensor(out=merged[:, of, 0:tp], in0=bat[:, 0:tp], in1=mgt[:, 0:tp], op=ALU.add), reads=[bat, mgt], writes=[merged])
            yield 'x'
            dma('sp', res[0:tp, :], xsrc[tok0:tok0 + tp, :], writes=[res])
            for blk in range(2):
                pa = nextA()
                for k in range(8):
                    op('pe', lambda t: t.matmul(pa[0:tp, :], lhsT=merged[:, k, 0:tp], rhs=wout[:, k, blk * 512:(blk + 1) * 512], start=(k == 0), stop=(k == 7)),
                       reads=[merged, wout], writes=[pa], inc=(k == 7))
                op('dve', lambda v: v.tensor_tensor(out=ftmp[0:tp, blk * 512:(blk + 1) * 512], in0=pa[0:tp, :], in1=gateB[0:tp, blk * 512:(blk + 1) * 512], op=ALU.mult),
                   reads=[pa, gateB], writes=[ftmp])
            op('dve', lambda v: v.tensor_tensor(out=res[0:tp, :], in0=res[0:tp, :], in1=ftmp[0:tp, :], op=ALU.add), reads=[res, ftmp], writes=[res])
            op('act', lambda a: a.activation(out=ftmp[0:tp, :], in_=res[0:tp, :], func=AF.Square, accum_out=st[0][0:tp, :]), reads=[res], writes=[ftmp, st[0]])
            op('dve', lambda v: v.tensor_scalar(out=st[1][0:tp, :], in0=st[0][0:tp, :], scalar1=1.0 / D, scalar2=1e-6, op0=ALU.mult, op1=ALU.add),
               reads=[st[0]], writes=[st[1]])
            op('act', lambda a: a.activation(out=st[1][0:tp, :], in_=st[1][0:tp, :], func=AF.Sqrt), reads=[st[1]], writes=[st[1]])
            op('dve', lambda v: v.reciprocal(out=st[1][0:tp, :], in_=st[1][0:tp, :]), reads=[st[1]], writes=[st[1]])
            op('dve', lambda v: v.scalar_tensor_tensor(out=ftmp[0:tp, :], in0=res[0:tp, :], scalar=st[1][0:tp, 0:1], in1=gfB[0:tp, :], op0=ALU.mult, op1=ALU.mult),
               reads=[res, st[1], gfB], writes=[ftmp])
            dma('act', y_out[tok0:tok0 + tp, :], ftmp[0:tp, :], reads=[ftmp])

        def begin_seq(s_, h0src):
            dma('sp', gateB[:, :], modscr[s_:s_ + 1, 2 * D:3 * D].to_broadcast([128, D]), reads=[scr_tok], writes=[gateB])
            if h0src is None:
                op('pool', lambda g: g.memset(hprev[:], 0.0), reads=hp_tok, writes=hp_tok)
            else:
                dma('sp', hprev[:, 0, :], h0src[0].rearrange("(j p) -> p j", p=128), writes=hp_tok)
                dma('sp', hprev[:, 1, :], h0src[1].rearrange("(j p) -> p j", p=128), writes=hp_tok)

        def end_seq(hr_dst, hi_dst):
            dma('act', hr_dst.rearrange("(j p) -> p j", p=128), hprev[:, 0, :], reads=hp_tok)
            dma('act', hi_dst.rearrange("(j p) -> p j", p=128), hprev[:, 1, :], reads=hp_tok)

        cs_tok = [Tok("cs%d" % i) for i in range(6)]

        def sample_prep(b_):
            op('pool', lambda g: g.memset(ftmp[0:1, 0:1], 0.0), writes=[ftmp])
            for kt in range(8):
                base = (kt % 2) * 3
                sk, ski, sv = cs_tok[base], cs_tok[base + 1], cs_tok[base + 2]
                ak = ftmp[:, base * 128:(base + 1) * 128]
                aki = ftmp[:, (base + 1) * 128:(base + 1) * 128 + 64]
                av = ftmp[:, (base + 2) * 128:(base + 3) * 128]
                dma('sp', ak, ck[b_, kt * 128:(kt + 1) * 128, :], reads=[ftmp], writes=[sk])
                dma('sp', aki, cki[b_, kt * 128:(kt + 1) * 128, :], reads=[ftmp], writes=[ski])
                dma('sp', av, cv[b_, kt * 128:(kt + 1) * 128, :], reads=[ftmp], writes=[sv])
                op('dve', lambda v: v.tensor_copy(out=kb2[:, :], in_=ak), reads=[sk, ftmp], writes=[kb2])
                pb_ = nextB()
                pbb = pb_[:].bitcast(BF16)
                op('pe', lambda t: t.transpose(out=pbb[:, 0:128], in_=kb2[:, :], identity=idb[:, :]), reads=[kb2, idb], writes=[pb_])
                op('act', lambda a: a.activation(out=kT[:, kt * 128:(kt + 1) * 128], in_=pbb[:, 0:128], func=AF.Copy), reads=[pb_], writes=[kT_tok[kt]])
                op('dve', lambda v: v.tensor_copy(out=kib[:, 0, :], in_=aki), reads=[ski, ftmp], writes=[kib])
                op('dve', lambda v: v.tensor_copy(out=kib[:, 1, :], in_=aki), reads=[ski, ftmp, kib], writes=[kib])
                pb_ = nextB()
                pbb2 = pb_[:].bitcast(BF16)
                op('pe', lambda t: t.transpose(out=pbb2[:, 0:128], in_=kib[:, :, :].rearrange("p a d -> p (a d)"), identity=idb[:, :]), reads=[kib, idb], writes=[pb_])
                op('act', lambda a: a.activation(out=kiT[:, kt * 128:(kt + 1) * 128], in_=pbb2[:, 0:128], func=AF.Copy), reads=[pb_], writes=[kiT_tok[kt]])
                op('pool', lambda v: v.tensor_copy(out=vaug[kt][:, :, 0:64], in_=av.rearrange("p (g d) -> p g d", d=64)), reads=[sv, ftmp], writes=[vaug[kt]])

        def drain(g):
            for _ in g:
                pass

        def run_until(g, tag):
            for t in g:
                if t == tag:
                    return True
            return False

        def interleave(f, b, cnext):
            ssm_mode['pc'] = False
            run_until(f, 'tok')
            run_until(b, 'bis')
            run_until(f, 'pre_ssm')
            fa = True
            nkt_ = (cnext[3] * 128 + 127) // 128 if cnext[1] else 9
            uy = min(8, int(nkt_ * 0.5 + 0.5))
            kdone = 0
            udone = 0
            while True:
                t = next(b, None)
                kdone += 1
                while fa and udone < uy and udone < (kdone * uy + nkt_ - 1) // max(nkt_, 1):
                    if next(f, None) is None:
                        fa = False
                    udone += 1
                if t is None or t == 'att_done':
                    break
            ssm_mode['pc'] = True
            g = idxgen(cnext)
            alive = [b, g] + ([f] if fa else [])
            while alive:
                for x_ in list(alive):
                    if next(x_, None) is None:
                        alive.remove(x_)
            ssm_mode['pc'] = False

        import os as _os
        ntile = [0]
        pend = None
        for b_ in range(0, 0 if _os.environ.get('DBG_SKIP_PROMPT') else 2):
            for qt in range(16):
                c = (b_, True, b_, qt, xp[b_], (yp[b_], kp[b_], vp[b_], kip[b_]), ntile[0] % 2)
                ntile[0] += 1
                pre = (lambda bb=b_: begin_seq(bb, None)) if qt == 0 else None
                if qt == 0 and pend is not None:
                    drain(pend)
                    pend = None
                f = front(c, pre)
                if pend is not None:
                    interleave(f, pend, c)
                else:
                    drain(f)
                    drain(idxgen(c))
                if qt == 15:
                    end_seq(hrp[b_, :], hip[b_, :])
                pend = back(c)
                if stage <= 5 + ntile[0] - 1:
                    drain(pend)
                    fw.barrier()
                    return nc
        if pend is not None:
            drain(pend)
            pend = None
        op('pool', lambda g: g.memset(negm[64:128, :], 0.0), reads=[negm], writes=[negm])
        for b_ in range(4):
            s_ = 2 + b_
            c = (s_, False, b_, 0, xs[b_], (ys[b_], ksm[b_], vsm[b_], kism[b_]), ntile[0] % 2)
            ntile[0] += 1

            def pre(bb=b_, ss=s_):
                begin_seq(ss, (h0r[bb, :], h0i[bb, :]))
                sample_prep(bb)
            if pend is not None:
                drain(pend)
                pend = None
            f = front(c, pre)
            drain(f)
            drain(idxgen(c))
            end_seq(hrs[b_, :], his[b_, :])
            pend = back(c)
        drain(pend)
        fw.barrier()
    return nc


_NC_CACHE = {}


def kernel(x_prompt, x_sample, cache_k, cache_v, cache_idx_k, state_ssm_re, state_ssm_im,
           c_prompt, c_sample, w_mod, b_mod, g_norm, w_in, lambda_re, lambda_im, log_dt,
           ssm_b_re, ssm_b_im, ssm_c_re, ssm_c_im, d_skip, w_glu, w_attn_proj, w_ssm_proj,
           w_out, g_final):
    f = lambda a: np.ascontiguousarray(np.asarray(a, dtype=np.float32))
    if 'nc' not in _NC_CACHE:
        _NC_CACHE['nc'] = build_nc()
    nc = _NC_CACHE['nc']
    x_prompt = f(x_prompt); x_sample = f(x_sample)
    cache_k = f(cache_k); cache_v = f(cache_v); cache_idx_k = f(cache_idx_k)
    state_ssm_re = f(state_ssm_re); state_ssm_im = f(state_ssm_im)
    c_prompt = f(c_prompt); c_sample = f(c_sample)
    shared = {
        "w_mod": f(w_mod)[0], "b_mod": f(b_mod), "g_norm": f(g_norm), "w_in": f(w_in)[0],
        "lam_re": f(lambda_re).reshape(1, 2048), "lam_im": f(lambda_im).reshape(1, 2048), "log_dt": f(log_dt),
        "b_re": f(ssm_b_re).reshape(2048, 16), "b_im": f(ssm_b_im).reshape(2048, 16),
        "c_re": f(ssm_c_re).reshape(512, 64), "c_im": f(ssm_c_im).reshape(512, 64),
        "d_skip": f(d_skip), "w_glu": f(w_glu)[0], "w_ap": f(w_attn_proj)[0], "w_sp": f(w_ssm_proj)[0],
        "w_out": f(w_out)[0], "g_final": f(g_final).reshape(1, D),
    }
    in_maps = []
    for c in range(NCORE):
        m = dict(shared)
        m["xp"] = x_prompt[2 * c:2 * c + 2]
        m["xs"] = x_sample[4 * c:4 * c + 4]
        m["ck"] = cache_k[0, 4 * c:4 * c + 4].reshape(4, PAST, 128)
        m["cv"] = cache_v[0, 4 * c:4 * c + 4].reshape(4, PAST, 128)
        m["cki"] = cache_idx_k[0, 4 * c:4 * c + 4]
        m["h0r"] = state_ssm_re[0, 4 * c:4 * c + 4].reshape(4, 2048)
        m["h0i"] = state_ssm_im[0, 4 * c:4 * c + 4].reshape(4, 2048)
        m["call"] = np.ascontiguousarray(np.concatenate([c_prompt[2 * c:2 * c + 2], c_sample[4 * c:4 * c + 4]], axis=0))
        in_maps.append({k: np.ascontiguousarray(v) for k, v in m.items()})
    if _NC_CACHE.get('dbg_one'):
        res = run_bass_kernel_spmd(nc, in_maps[:1], core_ids=[0])
        return res.results[0]
    res = run_bass_kernel_spmd(nc, in_maps, core_ids=list(range(NCORE)))
    R = res.results
    cat = lambda k: np.concatenate([np.asarray(r[k], dtype=np.float32) for r in R], axis=0)
    y_prompt = cat("yp")
    y_sample = cat("ys")
    k_prompt = cat("kp").reshape(1, 16, TP_, 2, 64)
    v_prompt = cat("vp").reshape(1, 16, TP_, 2, 64)
    ki_prompt = cat("kip").reshape(1, 16, TP_, 64)
    hr_p = cat("hrp").reshape(1, 16, 32, 64)
    hi_p = cat("hip").reshape(1, 16, 32, 64)
    k_s = cat("ksm").reshape(1, 32, TS_, 2, 64)
    v_s = cat("vsm").reshape(1, 32, TS_, 2, 64)
    ki_s = cat("kism").reshape(1, 32, TS_, 64)
    hr_s = cat("hrs").reshape(1, 32, 32, 64)
    hi_s = cat("his").reshape(1, 32, 32, 64)
    return (y_prompt, y_sample, k_prompt, v_prompt, ki_prompt, hr_p, hi_p, k_s, v_s, ki_s, hr_s, hi_s)
```
